# Optimizing a Trainium2 kernel written in Bass

```python
import jax
import jax.numpy as jnp
from jax import lax
import numpy as np

D_MODEL = 1024
BATCH = 16
SEQ = 2048
DEPTH = 4

N_MIXERS = 3
N_MOBA_LAYERS = (DEPTH + 2) // 3
N_HGRN_LAYERS = (DEPTH + 1) // 3
N_RGLRU_LAYERS = DEPTH // 3

MOBA_HEADS = 8
MOBA_HEAD_DIM = D_MODEL // MOBA_HEADS
MOBA_BLOCK = 256
MOBA_TOPK = 3
MOBA_Q_CHUNK = 64

HGRN_HEADS = 8
HGRN_KEY_DIM = 128
HGRN_FORGET_DIM = HGRN_HEADS * HGRN_KEY_DIM
HGRN_VAL_DIM = D_MODEL // HGRN_HEADS
HGRN_CHUNK = 64

RG_WIDTH = D_MODEL
RG_BLOCKS = 4
RG_BLOCK_WIDTH = RG_WIDTH // RG_BLOCKS
RG_CONV_WIDTH = 4
RG_C = 8.0

D_FF = 4 * D_MODEL
NORM_EPS = 1e-6

kernel_name = 'hybrid_moba_hgrn2_rglru_adaln'


def rms_norm(x, gain):
    xf = x.astype(jnp.float32)
    y = xf * lax.rsqrt(jnp.mean(xf * xf, axis=-1, keepdims=True) + NORM_EPS)
    return (y * gain.astype(jnp.float32)).astype(x.dtype)


def modulate(h, shift, scale):
    return h * (1.0 + scale[:, None, :]) + shift[:, None, :]


def sq_relu_mlp(h, w_up, w_down):
    u = jax.nn.relu(h @ w_up)
    return (u * u) @ w_down


def moba_attention(h, w_qkv, w_o):
    bsz, seq, _ = h.shape
    H, Dh, BLK, QC = MOBA_HEADS, MOBA_HEAD_DIM, MOBA_BLOCK, MOBA_Q_CHUNK
    q, k, v = jnp.split(h @ w_qkv, 3, axis=-1)
    n_blk = -(-seq // BLK)
    s_pad = n_blk * BLK

    def heads_padded(t):
        t = t.reshape(bsz, seq, H, Dh).transpose(0, 2, 1, 3)
        return jnp.pad(t, ((0, 0), (0, 0), (0, s_pad - seq), (0, 0)))

    q, k, v = heads_padded(q), heads_padded(k), heads_padded(v)
    kb = k.reshape(bsz, H, n_blk, BLK, Dh)
    vb = v.reshape(bsz, H, n_blk, BLK, Dh)
    n_sel = min(MOBA_TOPK, n_blk - 1)
    n_chunks = s_pad // QC
    q_chunks = q.reshape(bsz, H, n_chunks, QC, Dh).transpose(2, 0, 1, 3, 4)
    chunk_ids = jnp.arange(n_chunks)
    scale = Dh ** -0.5

    if n_sel > 0:
        q_blk = jnp.arange(s_pad) // BLK
        k_mean = jnp.mean(kb.astype(jnp.float32), axis=3)
        gate = jnp.einsum('bhsd,bhnd->bhsn', q.astype(jnp.float32), k_mean)
        fully_past = jnp.arange(n_blk)[None, :] < q_blk[:, None]
        gate = jnp.where(fully_past, gate, -jnp.inf)
        _, sel = lax.top_k(gate, n_sel)
        sel_chunks = sel.reshape(bsz, H, n_chunks, QC, n_sel).transpose(2, 0, 1, 3, 4)
        xs = (chunk_ids, q_chunks, sel_chunks)
    else:
        xs = (chunk_ids, q_chunks)

    b_idx = jnp.arange(bsz)[:, None, None, None]
    h_idx = jnp.arange(H)[None, :, None, None]

    def attend_chunk(args):
        ci, q_c = args[0], args[1]
        own = (ci * QC) // BLK
        q_pos = ci * QC + jnp.arange(QC)
        k_pos = own * BLK + jnp.arange(BLK)
        k_own = lax.dynamic_index_in_dim(kb, own, axis=2, keepdims=False)
        v_own = lax.dynamic_index_in_dim(vb, own, axis=2, keepdims=False)
        l_own = jnp.einsum('bhqd,bhkd->bhqk', q_c, k_own).astype(jnp.float32) * scale
        l_own = jnp.where(k_pos[None, :] <= q_pos[:, None], l_own, -jnp.inf)
        if n_sel == 0:
            p = jax.nn.softmax(l_own, axis=-1).astype(v_own.dtype)
            return jnp.einsum('bhqk,bhkd->bhqd', p, v_own)
        sel_c = args[2]
        k_sel = kb[b_idx, h_idx, sel_c]
        v_sel = vb[b_idx, h_idx, sel_c]
        l_sel = jnp.einsum('bhqd,bhqnkd->bhqnk', q_c, k_sel).astype(jnp.float32) * scale
        l_sel = jnp.where((sel_c < own)[..., None], l_sel, -jnp.inf)
        logits = jnp.concatenate([l_sel.reshape(bsz, H, QC, n_sel * BLK), l_own], axis=-1)
        p = jax.nn.softmax(logits, axis=-1).astype(v_own.dtype)
        p_sel = p[..., :n_sel * BLK].reshape(bsz, H, QC, n_sel, BLK)
        return (jnp.einsum('bhqnk,bhqnkd->bhqd', p_sel, v_sel)
                + jnp.einsum('bhqk,bhkd->bhqd', p[..., n_sel * BLK:], v_own))

    out = lax.map(attend_chunk, xs)
    out = out.transpose(1, 0, 3, 2, 4).reshape(bsz, s_pad, H * Dh)[:, :seq]
    return out @ w_o


def hgrn2_mixer(h, w_in, lb, g_gain, w_o):
    bsz, seq, _ = h.shape
    H, K, V, C = HGRN_HEADS, HGRN_KEY_DIM, HGRN_VAL_DIM, HGRN_CHUNK
    F = HGRN_FORGET_DIM
    q, f, i, g = jnp.split(h @ w_in, [F, 2 * F, 2 * F + H * V], axis=-1)
    q = jax.nn.silu(q.astype(jnp.float32))
    fgate = lb + (1.0 - lb) * jax.nn.sigmoid(f.astype(jnp.float32))
    k = 1.0 - fgate
    log_f = jnp.log(fgate)
    nc = seq // C

    def chunks(t, d):
        return t.reshape(bsz, nc, C, H, d).transpose(0, 3, 1, 2, 4)

    q, k, log_f = chunks(q, K), chunks(k, K), chunks(log_f, K)
    v = chunks(i.astype(jnp.float32), V)
    b = jnp.cumsum(log_f, axis=3)
    b_ref = b[:, :, :, C // 2:C // 2 + 1, :]
    b_last = b[:, :, :, C - 1:C, :]
    a = jnp.einsum('bhnck,bhnsk->bhncs', q * jnp.exp(b - b_ref), k * jnp.exp(b_ref - b))
    causal = jnp.tril(jnp.ones((C, C), dtype=bool))
    a = jnp.where(causal, a, 0.0)
    o_intra = jnp.einsum('bhncs,bhnsv->bhncv', a, v)
    q_in = q * jnp.exp(b)
    k_out = k * jnp.exp(b_last - b)
    decay = jnp.exp(b_last[:, :, :, 0, :])

    def step(state, xs):
        q_c, k_c, v_c, d_c = xs
        o = jnp.einsum('bhck,bhkv->bhcv', q_c, state)
        state = d_c[..., None] * state + jnp.einsum('bhck,bhcv->bhkv', k_c, v_c)
        return state, o

    xs = tuple(jnp.moveaxis(t, 2, 0) for t in (q_in, k_out, v, decay))
    state0 = jnp.zeros((bsz, H, K, V), jnp.float32)
    _, o_inter = lax.scan(step, state0, xs)
    o = o_intra + jnp.moveaxis(o_inter, 0, 2)
    o = o.transpose(0, 2, 3, 1, 4).reshape(bsz, seq, H, V)
    o = rms_norm(o, g_gain).reshape(bsz, seq, H * V) * jax.nn.silu(g.astype(jnp.float32))
    return o.astype(h.dtype) @ w_o


def rglru_mixer(h, w_in, conv_w, conv_b, w_a, b_a, w_i, b_i, lam, w_o):
    bsz, seq, _ = h.shape
    y_br, x_br = jnp.split(h @ w_in, 2, axis=-1)
    y_br = jax.nn.gelu(y_br)
    xp = jnp.pad(x_br, ((0, 0), (RG_CONV_WIDTH - 1, 0), (0, 0)))
    x_conv = conv_b
    for j in range(RG_CONV_WIDTH):
        x_conv = x_conv + xp[:, j:j + seq, :] * conv_w[j]
    xb = x_conv.reshape(bsz, seq, RG_BLOCKS, RG_BLOCK_WIDTH)
    r = jax.nn.sigmoid(jnp.einsum('bsnd,nde->bsne', xb, w_a).reshape(bsz, seq, RG_WIDTH) + b_a)
    gi = jax.nn.sigmoid(jnp.einsum('bsnd,nde->bsne', xb, w_i).reshape(bsz, seq, RG_WIDTH) + b_i)
    log_a = -RG_C * r.astype(jnp.float32) * jax.nn.softplus(-lam.astype(jnp.float32))
    a = jnp.exp(log_a)
    mult = jnp.sqrt(-jnp.expm1(2.0 * log_a))
    first = (jnp.arange(seq) == 0)[None, :, None]
    mult = jnp.where(first, 1.0, mult)
    u = (gi * x_conv).astype(jnp.float32) * mult

    def combine(left, right):
        return left[0] * right[0], right[0] * left[1] + right[1]

    _, hs = lax.associative_scan(combine, (a, u), axis=1)
    return (hs.astype(h.dtype) * y_br) @ w_o


def setup_inputs(seed: int = 0) -> dict:
    key = jax.random.key(seed)
    ks = jax.random.split(key, 24)
    f32 = jnp.float32
    D = D_MODEL

    def nrm(k, shape, scale):
        return jax.random.normal(k, shape, f32) * scale

    a_target = jax.random.uniform(ks[21], (N_RGLRU_LAYERS, RG_WIDTH), f32, 0.9, 0.999)
    s_root = a_target ** (1.0 / RG_C)
    return {
        'x': nrm(ks[0], (BATCH, SEQ, D), 1.0),
        'c': nrm(ks[1], (BATCH, D), 1.0),
        'ada_w': nrm(ks[2], (DEPTH, D, 6 * D), 0.5 * D ** -0.5),
        'ada_b': nrm(ks[3], (DEPTH, 6 * D), 0.02),
        'norm_mix': 1.0 + nrm(ks[4], (DEPTH, D), 0.02),
        'norm_mlp': 1.0 + nrm(ks[5], (DEPTH, D), 0.02),
        'mlp_up': nrm(ks[6], (DEPTH, D, D_FF), D ** -0.5),
        'mlp_down': nrm(ks[7], (DEPTH, D_FF, D), D_FF ** -0.5),
        'moba_wqkv': nrm(ks[8], (N_MOBA_LAYERS, D, 3 * MOBA_HEADS * MOBA_HEAD_DIM), D ** -0.5),
        'moba_wo': nrm(ks[9], (N_MOBA_LAYERS, MOBA_HEADS * MOBA_HEAD_DIM, D), D ** -0.5),
        'hgrn_w_in': nrm(ks[10], (N_HGRN_LAYERS, D, 2 * HGRN_FORGET_DIM + 2 * HGRN_HEADS * HGRN_VAL_DIM), D ** -0.5),
        'hgrn_lb': nrm(ks[11], (DEPTH, HGRN_FORGET_DIM), 1.0),
        'hgrn_norm': 1.0 + nrm(ks[12], (N_HGRN_LAYERS, HGRN_VAL_DIM), 0.02),
        'hgrn_wo': nrm(ks[13], (N_HGRN_LAYERS, HGRN_HEADS * HGRN_VAL_DIM, D), D ** -0.5),
        'rg_w_in': nrm(ks[14], (N_RGLRU_LAYERS, D, 2 * RG_WIDTH), D ** -0.5),
        'rg_conv_w': nrm(ks[15], (N_RGLRU_LAYERS, RG_CONV_WIDTH, RG_WIDTH), RG_CONV_WIDTH ** -0.5),
        'rg_conv_b': nrm(ks[16], (N_RGLRU_LAYERS, RG_WIDTH), 0.02),
        'rg_w_a': nrm(ks[17], (N_RGLRU_LAYERS, RG_BLOCKS, RG_BLOCK_WIDTH, RG_BLOCK_WIDTH), RG_BLOCK_WIDTH ** -0.5),
        'rg_b_a': nrm(ks[18], (N_RGLRU_LAYERS, RG_WIDTH), 0.02),
        'rg_w_i': nrm(ks[19], (N_RGLRU_LAYERS, RG_BLOCKS, RG_BLOCK_WIDTH, RG_BLOCK_WIDTH), RG_BLOCK_WIDTH ** -0.5),
        'rg_b_i': nrm(ks[20], (N_RGLRU_LAYERS, RG_WIDTH), 0.02),
        'rg_lambda': jnp.log(s_root) - jnp.log1p(-s_root),
        'rg_wo': nrm(ks[22], (N_RGLRU_LAYERS, RG_WIDTH, D), RG_WIDTH ** -0.5),
        'final_norm': 1.0 + nrm(ks[23], (D,), 0.02),
    }


def reference(x, c, ada_w, ada_b, norm_mix, norm_mlp, mlp_up, mlp_down,
              moba_wqkv, moba_wo, hgrn_w_in, hgrn_lb, hgrn_norm, hgrn_wo,
              rg_w_in, rg_conv_w, rg_conv_b, rg_w_a, rg_b_a, rg_w_i, rg_b_i,
              rg_lambda, rg_wo, final_norm):
    cond = jax.nn.silu(c)
    lb_all = jnp.cumsum(jax.nn.softmax(hgrn_lb.astype(jnp.float32), axis=0), axis=0)
    lb_all = lb_all - lb_all[0:1]
    i_a = 0
    i_b = 0
    i_c = 0
    for layer in range(DEPTH):
        mod = cond @ ada_w[layer] + ada_b[layer]
        shift1, scale1, gate1, shift2, scale2, gate2 = jnp.split(mod, 6, axis=-1)
        h = modulate(rms_norm(x, norm_mix[layer]), shift1, scale1)
        kind = layer % N_MIXERS
        if kind == 0:
            y = moba_attention(h, moba_wqkv[i_a], moba_wo[i_a])
            i_a += 1
        elif kind == 1:
            y = hgrn2_mixer(h, hgrn_w_in[i_b], lb_all[layer], hgrn_norm[i_b], hgrn_wo[i_b])
            i_b += 1
        else:
            y = rglru_mixer(h, rg_w_in[i_c], rg_conv_w[i_c], rg_conv_b[i_c], rg_w_a[i_c],
                            rg_b_a[i_c], rg_w_i[i_c], rg_b_i[i_c], rg_lambda[i_c], rg_wo[i_c])
            i_c += 1
        x = x + gate1[:, None, :] * y
        h = modulate(rms_norm(x, norm_mlp[layer]), shift2, scale2)
        x = x + gate2[:, None, :] * sq_relu_mlp(h, mlp_up[layer], mlp_down[layer])
    return rms_norm(x, final_norm)
```

```python
import numpy as np
import concourse.bass as bass
import concourse.mybir as mybir
from concourse.bass_utils import run_bass_kernel_spmd

F32 = mybir.dt.float32
BF16 = mybir.dt.bfloat16
ALU = mybir.AluOpType
AF = mybir.ActivationFunctionType
AX = mybir.AxisListType

STRICT_SAME_ENGINE = True


class Buf:
    def __init__(self, fw, name, t):
        self.fw = fw
        self.name = name
        self.t = t
        self.last_w = None
        self.reads = {}
        self.dsem = None
        self.dcount = 0
        self.is_psum = False

    def __getitem__(self, idx):
        return V(self.t[idx], self)

    def ap(self, offset, pat):
        return V(bass.AP(self.t, offset, pat), self)


class V:
    def __init__(self, ap, buf):
        self.ap = ap
        self.buf = buf

    def __getitem__(self, idx):
        return V(self.ap[idx], self.buf)

    def re(self, pat, **kw):
        return V(self.ap.rearrange(pat, **kw), self.buf)

    def bc(self, shape):
        return V(self.ap.broadcast_to(list(shape)), self.buf)


class SemObj:
    def __init__(self, h, name):
        self.h = h
        self.name = name


class Eng:
    def __init__(self, fw, name, eng):
        self.fw = fw
        self.name = name
        self.eng = eng
        self.sem = None
        self.count = 0
        self.waited = {}
        self.pend_r = []
        self.pend_w = []
        self.nwait = 0
        self.nins = 0
        self.prog = []

    def wait_ev(self, ev):
        if ev is None:
            return
        s, v = ev
        if s is self.sem and (not STRICT_SAME_ENGINE or self.name == "pe"):
            return
        if self.waited.get(s, 0) >= v:
            return
        eng = self.eng
        self.prog.append(lambda: eng.wait_ge(s.h, v))
        self.waited[s] = v
        self.nwait += 1


class FW:
    def __init__(self, nc, stack):
        self.nc = nc
        self.stack = stack
        self.pe = self._mk("pe", nc.tensor)
        self.act = self._mk("act", nc.scalar)
        self.dve = self._mk("dve", nc.vector)
        self.pool = self._mk("pool", nc.gpsimd)
        self.sp = self._mk("sp", nc.sync)
        self.nsem = 5

    def _mk(self, name, eng):
        e = Eng(self, name, eng)
        e.sem = SemObj(self.stack.enter_context(self.nc.semaphore("s_" + name)), name)
        return e

    def sbuf(self, name, shape, dt):
        t = self.stack.enter_context(self.nc.sbuf_tensor(name, list(shape), dt))
        return Buf(self, name, t)

    def psum(self, name, shape, dt=F32):
        t = self.stack.enter_context(self.nc.psum_tensor(name, list(shape), dt))
        b = Buf(self, name, t)
        b.is_psum = True
        return b

    def op(self, E, fn, outs, ins, inc=True):
        for v in ins:
            E.wait_ev(v.buf.last_w)
            if v.buf.is_psum:
                for s, val in list(v.buf.reads.items()):
                    if s is not E.sem:
                        E.wait_ev((s, val))
        for v in outs:
            E.wait_ev(v.buf.last_w)
            for s, val in list(v.buf.reads.items()):
                E.wait_ev((s, val))
        E.nins += 1
        E.pend_r.extend(v.buf for v in ins)
        E.pend_w.extend(v.buf for v in outs)
        if inc:
            semh = E.sem.h
            E.prog.append(lambda: fn().then_inc(semh, 1))
        else:
            E.prog.append(fn)
        if inc:
            E.count += 1
            ev = (E.sem, E.count)
            for b in E.pend_r:
                if b.reads.get(E.sem, 0) < E.count:
                    b.reads[E.sem] = E.count
            for b in E.pend_w:
                b.last_w = ev
                b.reads = {}
            E.pend_r = []
            E.pend_w = []
        return None

    def dma(self, E, out, in_, out_dram=False, in_dram=False, **kw):
        tracked = None
        if not in_dram:
            E.wait_ev(in_.buf.last_w)
            tracked = in_.buf
        if not out_dram:
            E.wait_ev(out.buf.last_w)
            for s, val in list(out.buf.reads.items()):
                E.wait_ev((s, val))
            tracked = out.buf
        b = tracked
        if b.dsem is None:
            b.dsem = SemObj(self.stack.enter_context(self.nc.semaphore("d_" + b.name)), "d_" + b.name)
            self.nsem += 1
        oap = out if out_dram else out.ap
        iap = in_ if in_dram else in_.ap
        eng = E.eng
        dh = b.dsem.h
        E.prog.append(lambda: eng.dma_start(out=oap, in_=iap, **kw).then_inc(dh, 16))
        b.dcount += 16
        ev = (b.dsem, b.dcount)
        E.nins += 1
        if not out_dram:
            out.buf.last_w = ev
            out.buf.reads = {}
        if not in_dram:
            if in_.buf.reads.get(b.dsem, 0) < b.dcount:
                in_.buf.reads[b.dsem] = b.dcount
        return ev

    def mm(self, out, lhsT, rhs, start=True, stop=True, inc=None):
        if inc is None:
            inc = stop
        return self.op(self.pe, lambda: self.nc.tensor.matmul(out.ap, lhsT.ap, rhs.ap, start=start, stop=stop),
                       [out], [lhsT, rhs], inc=inc)

    def transpose(self, out, in_, ident):
        return self.op(self.pe, lambda: self.nc.tensor.transpose(out.ap, in_.ap, ident.ap), [out], [in_, ident])

    def actf(self, out, in_, func, bias=None, scale=None, accum_out=None, E=None):
        ins = [in_]
        kw = {}
        if bias is not None:
            if isinstance(bias, V):
                ins.append(bias); kw["bias"] = bias.ap
            else:
                kw["bias"] = bias
        if scale is not None:
            if isinstance(scale, V):
                ins.append(scale); kw["scale"] = scale.ap
            else:
                kw["scale"] = scale
        outs = [out]
        if accum_out is not None:
            outs.append(accum_out); kw["accum_out"] = accum_out.ap
        return self.op(self.act, lambda: self.nc.scalar.activation(out.ap, in_.ap, func, **kw), outs, ins)

    def _ve(self, E):
        return E if E is not None else self.dve

    def tt(self, out, in0, in1, op, E=None):
        E = self._ve(E)
        return self.op(E, lambda: E.eng.tensor_tensor(out.ap, in0.ap, in1.ap, op), [out], [in0, in1])

    def ts(self, out, in0, s1, s2, op0, op1=None, E=None, accum_out=None):
        E = self._ve(E)
        ins = [in0]
        a1 = s1
        if isinstance(s1, V):
            ins.append(s1); a1 = s1.ap
        a2 = s2
        if isinstance(s2, V):
            ins.append(s2); a2 = s2.ap
        kw = {}
        outs = [out]
        if op1 is not None:
            kw["op1"] = op1
        if accum_out is not None:
            kw["accum_out"] = accum_out.ap; outs.append(accum_out)
        return self.op(E, lambda: E.eng.tensor_scalar(out.ap, in0.ap, a1, a2, op0, **kw), outs, ins)

    def stt(self, out, in0, scalar, in1, op0, op1):
        ins = [in0, in1]
        a = scalar
        if isinstance(scalar, V):
            ins.append(scalar); a = scalar.ap
        return self.op(self.dve, lambda: self.nc.vector.scalar_tensor_tensor(out.ap, in0.ap, a, in1.ap, op0, op1),
                       [out], ins)

    def copy(self, out, in_, E=None):
        E = self._ve(E)
        if E is self.act:
            return self.op(E, lambda: self.nc.scalar.copy(out.ap, in_.ap), [out], [in_])
        return self.op(E, lambda: E.eng.tensor_copy(out.ap, in_.ap), [out], [in_])

    def memset(self, out, val, E=None):
        E = self._ve(E)
        return self.op(E, lambda: E.eng.memset(out.ap, val), [out], [])

    def reduce(self, out, in_, op, axis=AX.X):
        return self.op(self.dve, lambda: self.nc.vector.tensor_reduce(out.ap, in_.ap, axis, op), [out], [in_])

    def copy_pred(self, out, mask, data):
        return self.op(self.dve, lambda: self.nc.vector.copy_predicated(out.ap, mask.ap, data.ap), [out], [mask, data])

    def recip(self, out, in_):
        return self.op(self.dve, lambda: self.nc.vector.reciprocal(out.ap, in_.ap), [out], [in_])

    def max8(self, out, in_):
        return self.op(self.dve, lambda: self.nc.vector.max(out.ap, in_.ap), [out], [in_])

    def scan(self, out, d0, d1, initial, op0, op1):
        ins = [d0, d1]
        a = initial
        if isinstance(initial, V):
            ins.append(initial); a = initial.ap
        return self.op(self.dve, lambda: self.nc.vector.tensor_tensor_scan(out.ap, d0.ap, d1.ap, a, op0, op1),
                       [out], ins)

    def run(self):
        with self.nc.Block() as block:
            def mk(E):
                def body(_e):
                    for c in E.prog:
                        c()
                return body
            block.tensor(mk(self.pe))
            block.scalar(mk(self.act))
            block.vector(mk(self.dve))
            block.gpsimd(mk(self.pool))
            block.sync(mk(self.sp))

    def finish(self, bufs):
        for b in bufs:
            self.sp.wait_ev(b.last_w)
            for s, val in list(b.reads.items()):
                self.sp.wait_ev((s, val))

import contextlib

D = 1024
S = 2048
TG = 512
NTG = 4
DFF = 4096
EPS = 1e-6
NSLOT = 4

C_COND = 0
C_ADAB = 16
C_NMIX = 208
C_NMLP = 240
C_FIN = 272
C_LB = 280
C_GN = 312
C_CW = 313
C_CB = 345
C_BA = 353
C_BI = 361
C_LAM = 369
NCONST = 384
NEG = -1.0e30
MASKV = 30000.0


class K:
    pass


LAST_FW = None
DEBUG = False


def build(layers=(0, 1, 2, 3), nseq=2, final=True, dbg=None):
    nc = bass.Bass("TRN2", target_bir_lowering=False)
    k = K()
    k.nc = nc
    dt = nc.dram_tensor
    k.xT_d = dt("xT", [2, D, S], F32, kind="ExternalInput").ap()
    k.consts_d = dt("consts", [128, NCONST], F32, kind="ExternalInput").ap()
    k.ada_w = dt("ada_w", [4, D, 6 * D], F32, kind="ExternalInput").ap()
    k.mlp_up = dt("mlp_up", [4, D, DFF], F32, kind="ExternalInput").ap()
    k.mlp_down = dt("mlp_down", [4, DFF, D], F32, kind="ExternalInput").ap()
    k.moba_wqkv = dt("moba_wqkv", [2, 8, D, 384], F32, kind="ExternalInput").ap()
    k.moba_wo = dt("moba_wo", [2, D, D], F32, kind="ExternalInput").ap()
    k.hgrn_w_in = dt("hgrn_w_in", [1, 8, D, 512], F32, kind="ExternalInput").ap()
    k.hgrn_wo = dt("hgrn_wo", [1, D, D], F32, kind="ExternalInput").ap()
    k.rg_w_in = dt("rg_w_in", [1, 4, D, 512], F32, kind="ExternalInput").ap()
    k.rg_w_a = dt("rg_w_a", [1, 4, 256, 256], F32, kind="ExternalInput").ap()
    k.rg_w_i = dt("rg_w_i", [1, 4, 256, 256], F32, kind="ExternalInput").ap()
    k.rg_wo = dt("rg_wo", [1, D, D], F32, kind="ExternalInput").ap()
    k.outT_d = dt("outT", [2, D, S], F32, kind="ExternalOutput").ap()
    k.dbg_d = dt("dbg", [8, 128, 2080], F32, kind="ExternalOutput").ap() if DEBUG else None

    with contextlib.ExitStack() as st:
        fw = FW(nc, st)
        k.fw = fw
        global LAST_FW
        LAST_FW = fw
        k.xT = [fw.sbuf(f"xT{c}", [128, S], F32) for c in range(8)]
        k.hT = fw.sbuf("hT", [128, 8, S], BF16)
        k.big = [fw.sbuf(f"big{i}", [128, 4, S], BF16) for i in range(2)]
        k.slots = [fw.sbuf(f"ws{i}", [128, 4096], BF16) for i in range(NSLOT)]
        k.slot_i = 0
        k.cst = fw.sbuf("cst", [128, NCONST], F32)
        k.condT = fw.sbuf("condT", [128, 8, 2], BF16)
        k.modT = fw.sbuf("modT", [128, 4, 48, 2], F32)
        k.coef = fw.sbuf("coef", [128, 4, 2, 6, 8], F32)
        k.misc = fw.sbuf("misc", [128, 64], F32)
        k.ident = fw.sbuf("ident", [128, 128], F32)
        k.epsb = fw.sbuf("epsb", [128, 1], F32)
        k.onesb = fw.sbuf("onesb", [128, 128], BF16)
        k.trib = fw.sbuf("trib", [128, 128], BF16)
        k.hmask = fw.sbuf("hmask", [128, 128], F32)
        k.negm = fw.sbuf("negm", [128, 16, 8], F32)
        k.onehot = fw.sbuf("onehot", [8, 8, 128], BF16)
        k.identb = fw.sbuf("identb", [128, 128], BF16)
        k.tribias = fw.sbuf("tribias", [128, 128], BF16)
        k.scr = fw.sbuf("scr", [128, 9648], F32)
        k.banks = [fw.psum(f"pb{i}", [128, 512], F32) for i in range(8)]
        k.bank_i = 0
        k.nrot = 8
        k.scr_off = 0
        k.ph = 0

        emit_setup(k)
        first = True
        for s in range(nseq):
            load_x(k, s)
            for l in layers:
                on = lambda n: dbg is None or n in dbg
                if s == 0 and first and on("mod"):
                    emit_mod(k, l)
                    first = False
                if on("norm0"):
                    emit_norm(k, l, s, 0)
                kind = l % 3
                if on("mix"):
                    if kind == 0:
                        emit_moba(k, l, s)
                    elif kind == 1:
                        emit_hgrn(k, l, s)
                    else:
                        emit_rglru(k, l, s)
                if s == 0 and on("mod"):
                    nxt = [m for m in layers if m > l]
                    if nxt:
                        emit_mod(k, nxt[0])
                if on("norm1"):
                    emit_norm(k, l, s, 1)
                if on("mlp"):
                    emit_mlp(k, l, s)
            emit_final(k, s, final)
        fw.finish(k.xT)
        fw.run()
    return nc


def bank(k):
    n = k.nrot
    b = k.banks[k.bank_i % n]
    k.bank_i += 1
    return b


def wslot(k):
    b = k.slots[k.slot_i % NSLOT]
    k.slot_i += 1
    return b


def barrier(k):
    fw = k.fw
    engs = [fw.pe, fw.act, fw.dve, fw.pool]
    for E in engs:
        for X in engs:
            if X is not E and X.count > 0:
                E.wait_ev((X.sem, X.count))


def phase_begin(k):
    barrier(k)
    k.scr_off = 0
    k.ph += 1


def salloc(k, name, n, dtype=F32):
    nf = n if dtype == F32 else (n + 1) // 2
    nf = (nf + 7) // 8 * 8
    assert k.scr_off + nf <= 9648, (name, k.scr_off, nf)
    ap = k.scr.t[:, k.scr_off:k.scr_off + nf]
    k.scr_off += nf
    if dtype != F32:
        ap = ap.bitcast(dtype)[:, 0:n]
    else:
        ap = ap[:, 0:n]
    return Buf(k.fw, f"{name}_{k.ph}", ap)


def cs(k, col, n=1):
    return k.cst[:, col:col + n]


def coef(k, l, s, j, c):
    return k.coef[:, l, s, j, c:c + 1]


def emit_setup(k):
    fw, nc = k.fw, k.nc
    fw.dma(fw.sp, k.cst[:], k.consts_d, in_dram=True)
    fw.memset(k.epsb[:], EPS, E=fw.pool)
    fw.memset(k.ident[:], 1.0, E=fw.pool)
    fw.op(fw.pool, lambda: nc.gpsimd.affine_select(k.ident.t[:], k.ident.t[:], [[-1, 128]], ALU.is_equal, 0.0,
                                                   base=0, channel_multiplier=1), [k.ident[:]], [k.ident[:]])
    fw.memset(k.onesb[:], 1.0, E=fw.pool)
    fw.memset(k.hmask[:], 1.0, E=fw.pool)
    fw.op(fw.pool, lambda: nc.gpsimd.affine_select(k.hmask.t[:], k.hmask.t[:], [[1, 128]], ALU.is_ge, 0.0,
                                                   base=0, channel_multiplier=-1), [k.hmask[:]], [k.hmask[:]])
    fw.copy(k.trib[:], k.hmask[:], E=fw.pool)
    fw.ts(k.tribias[:], k.hmask[:], MASKV, -MASKV, ALU.mult, ALU.add)
    fw.copy(k.identb[:], k.ident[:])
    fw.memset(k.onehot[:], 1.0, E=fw.pool)
    fw.op(fw.pool, lambda: nc.gpsimd.affine_select(k.onehot.t[:], k.onehot.t[:], [[1, 8], [0, 128]], ALU.is_equal, 0.0,
                                                   base=0, channel_multiplier=-1), [k.onehot[:]], [k.onehot[:]])
    fw.memset(k.hmask[0:64, 64:128], 0.0, E=fw.pool)
    fw.memset(k.negm[:], 0.0, E=fw.pool)
    for tt in range(16):
        j = tt // 2
        fw.memset(k.negm[:, tt, j:8], NEG, E=fw.pool)
    fw.actf(k.condT[:].re("p a b -> p (a b)"), cs(k, C_COND, 16), AF.Silu)
    m = k.misc
    lbv = k.cst[:, C_LB:C_LB + 32]
    e = k.misc[:, 0:32]
    fw.actf(e, lbv, AF.Exp)
    ssum = k.misc[:, 32:40]
    fw.tt(ssum, k.misc[:, 0:8], k.misc[:, 8:16], ALU.add)
    fw.tt(ssum, ssum, k.misc[:, 16:24], ALU.add)
    fw.tt(ssum, ssum, k.misc[:, 24:32], ALU.add)
    fw.op(fw.dve, lambda: nc.vector.reciprocal(k.misc.t[:, 32:40], k.misc.t[:, 32:40]), [ssum], [ssum])
    k.lb1 = k.misc[:, 40:48]
    k.oml1 = k.misc[:, 48:56]
    fw.tt(k.lb1, k.misc[:, 8:16], ssum, ALU.mult)
    fw.ts(k.oml1, k.lb1, -1.0, 1.0, ALU.mult, ALU.add)
    k.hc = fw.sbuf("hc", [128, 16], F32)
    fw.ts(k.hc[:, 8:16], k.oml1, 0.5, None, ALU.mult)
    fw.ts(k.hc[:, 0:8], k.hc[:, 8:16], -1.0, 1.0, ALU.mult, ALU.add)
    k.cl = k.misc[:, 56:64]
    tmp = k.misc[:, 0:8]
    fw.actf(tmp, cs(k, C_LAM, 8), AF.Exp, scale=-1.0)
    fw.ts(tmp, tmp, 1.0, None, ALU.add)
    fw.actf(tmp, tmp, AF.Ln)
    fw.ts(k.cl, tmp, -8.0, None, ALU.mult)
    k.ncl = fw.sbuf("ncl", [128, 8], F32)
    fw.ts(k.ncl[:], tmp, 8.0, None, ALU.mult)


def load_x(k, s):
    fw = k.fw
    src = k.xT_d[s].rearrange("(c p) t -> c p t", p=128)
    for c in range(8):
        fw.dma(fw.sp, k.xT[c][:], src[c], in_dram=True)


def emit_mod(k, l):
    fw, nc = k.fw, k.nc
    pb = bank(k)
    for nb in range(12):
        sl, wv = load_w(k, k.ada_w[l][:, nb * 512:(nb + 1) * 512].rearrange("(kc p) n -> p kc n", p=128),
                        lambda a: a.rearrange("p (kc n) -> p kc n", kc=8))
        for n4 in range(4):
            n = nb * 4 + n4
            for kc in range(8):
                fw.mm(pb[:, 2 * n:2 * n + 2], V(wv[:, kc, n4 * 128:(n4 + 1) * 128], sl), k.condT[:, kc, :],
                      start=(kc == 0), stop=(kc == 7))
    for s in range(2):
        src = pb[:, 0:96].ap.rearrange("p (n s) -> p n s", s=2)[:, :, s]
        fw.tt(k.modT[:, l, :, s], V(src, pb), cs(k, C_ADAB + l * 48, 48), ALU.add)
    for s in range(2):
        md = lambda j: k.modT[:, l, j * 8:(j + 1) * 8, s]
        fw.stt(k.coef[:, l, s, 0, :], md(1), 1.0, cs(k, C_NMIX + l * 8, 8), ALU.add, ALU.mult)
        fw.copy(k.coef[:, l, s, 1, :], md(0))
        fw.copy(k.coef[:, l, s, 2, :], md(2))
        fw.stt(k.coef[:, l, s, 3, :], md(4), 1.0, cs(k, C_NMLP + l * 8, 8), ALU.add, ALU.mult)
        fw.copy(k.coef[:, l, s, 4, :], md(3))
        fw.copy(k.coef[:, l, s, 5, :], md(5))


def rstd_from_ss(k, ss_ps, rt, rs, n, width):
    fw, nc = k.fw, k.nc
    fw.actf(rt, ss_ps, AF.Ln, bias=k.epsb[:, 0:1], scale=1.0 / n)
    fw.actf(rs, rt, AF.Exp, scale=-0.5)


def emit_norm(k, l, s, which):
    fw, nc = k.fw, k.nc
    phase_begin(k)
    sq = [salloc(k, f"sq{i}", TG, BF16) for i in range(8)]
    rt = salloc(k, "rt", TG)
    rs = [salloc(k, f"rs{i}", TG) for i in range(2)]
    tmp = [salloc(k, f"tmp{i}", TG) for i in range(4)]
    ja, jb = (0, 1) if which == 0 else (3, 4)
    for tg in range(NTG):
        tsl = slice(tg * TG, (tg + 1) * TG)
        pb = bank(k)
        for c in range(8):
            q = sq[c]
            fw.tt(q[:], k.xT[c][:, tsl], k.xT[c][:, tsl], ALU.mult)
            fw.mm(pb[:], k.onesb[:], q[:], start=(c == 0), stop=(c == 7))
        r = rs[tg % 2]
        rstd_from_ss(k, pb[:], rt[:], r[:], D, TG)
        for c in range(8):
            t = tmp[c % 4]
            fw.stt(t[:], k.xT[c][:, tsl], coef(k, l, s, ja, c), r[:], ALU.mult, ALU.mult)
            fw.actf(k.hT[:, c, tsl], t[:], AF.Identity, bias=coef(k, l, s, jb, c))


def emit_final(k, s, final):
    fw, nc = k.fw, k.nc
    phase_begin(k)
    if final:
        sq = [salloc(k, f"sq{i}", TG, BF16) for i in range(8)]
        rt = salloc(k, "rt", TG)
        rs = [salloc(k, f"rs{i}", TG) for i in range(2)]
        for tg in range(NTG):
            tsl = slice(tg * TG, (tg + 1) * TG)
            pb = bank(k)
            for c in range(8):
                q = sq[c]
                fw.tt(q[:], k.xT[c][:, tsl], k.xT[c][:, tsl], ALU.mult)
                fw.mm(pb[:], k.onesb[:], q[:], start=(c == 0), stop=(c == 7))
            r = rs[tg % 2]
            rstd_from_ss(k, pb[:], rt[:], r[:], D, TG)
            for c in range(8):
                fw.stt(k.xT[c][:, tsl], k.xT[c][:, tsl], cs(k, C_FIN + c), r[:], ALU.mult, ALU.mult)
    dst = k.outT_d[s].rearrange("(c p) t -> c p t", p=128)
    for c in range(8):
        fw.dma(fw.sp, dst[c], k.xT[c][:], out_dram=True)


def load_w(k, src_ap, view):
    fw = k.fw
    sl = wslot(k)
    v = view(sl.t[:, :])
    fw.dma(fw.pool, V(v, sl), src_ap, in_dram=True)
    return sl, v


def out_proj(k, wo_ap, l, s):
    fw = k.fw
    for half in range(2):
        sl, wv = load_w(k, wo_ap[:, half * 512:(half + 1) * 512].rearrange("(h p) n -> p h n", p=128),
                        lambda a: a.rearrange("p (h n) -> p h n", h=8))
        for dc4 in range(4):
            dc = half * 4 + dc4
            for tg in range(NTG):
                tsl = slice(tg * TG, (tg + 1) * TG)
                pb = bank(k)
                for h in range(8):
                    fw.mm(pb[:], V(wv[:, h, dc4 * 128:(dc4 + 1) * 128], sl), k.big[h // 4][:, h % 4, tsl],
                          start=(h == 0), stop=(h == 7))
                fw.stt(k.xT[dc][:, tsl], pb[:], coef(k, l, s, 2, dc), k.xT[dc][:, tsl], ALU.mult, ALU.add)


def emit_mlp(k, l, s):
    fw, nc = k.fw, k.nc
    phase_begin(k)
    rl = [salloc(k, f"rl{i}", TG) for i in range(4)]
    ri = 0

    def load(fb):
        a = load_w(k, k.mlp_up[l][:, fb * 512:(fb + 1) * 512].rearrange("(kc p) n -> p kc n", p=128),
                   lambda a: a.rearrange("p (kc n) -> p kc n", kc=8))
        b = load_w(k, k.mlp_down[l][fb * 512:(fb + 1) * 512, :].rearrange("(kc p) n -> p kc n", p=128),
                   lambda a: a.rearrange("p (kc n) -> p kc n", kc=4))
        return a, b

    def up(fb, wu):
        nonlocal ri
        sl, wv = wu
        u = k.big[fb % 2]
        for fc in range(4):
            for tg in range(NTG):
                tsl = slice(tg * TG, (tg + 1) * TG)
                pb = bank(k)
                for kc in range(8):
                    fw.mm(pb[:], V(wv[:, kc, fc * 128:(fc + 1) * 128], sl), k.hT[:, kc, tsl],
                          start=(kc == 0), stop=(kc == 7))
                r = rl[ri % 4]
                ri += 1
                fw.actf(r[:], pb[:], AF.Relu)
                fw.tt(u[:, fc, tsl], r[:], r[:], ALU.mult)

    def down(fb, wd):
        sl, wv = wd
        u = k.big[fb % 2]
        for dc in range(8):
            for tg in range(NTG):
                tsl = slice(tg * TG, (tg + 1) * TG)
                pb = bank(k)
                for fc in range(4):
                    fw.mm(pb[:], V(wv[:, fc, dc * 128:(dc + 1) * 128], sl), u[:, fc, tsl],
                          start=(fc == 0), stop=(fc == 3))
                fw.stt(k.xT[dc][:, tsl], pb[:], coef(k, l, s, 5, dc), k.xT[dc][:, tsl], ALU.mult, ALU.add)

    ws = {}
    ws[0] = load(0)
    up(0, ws[0][0])
    for fb in range(8):
        if fb + 1 < 8:
            ws[fb + 1] = load(fb + 1)
            up(fb + 1, ws[fb + 1][0])
        down(fb, ws[fb][1])


MOBA_STAGE = 99


def dump(k, i, view, n):
    if DEBUG:
        k.fw.dma(k.fw.pool, k.dbg_d[i][:, 0:n], view, out_dram=True)


SKIP = set()
MOBA_HEADS = 8


def emit_moba(k, l, s):
    fw, nc = k.fw, k.nc
    la = l // 3
    phase_begin(k)
    k.nrot = 5
    obk = [k.banks[5], k.banks[6]]
    gps = k.banks[7]
    qf = salloc(k, "qf", TG)
    qb = [salloc(k, f"qb{i}", S, BF16) for i in range(2)]
    kb = [salloc(k, f"kb{i}", S, BF16) for i in range(2)]
    vb = [salloc(k, f"vb{i}", 16 * 130, BF16) for i in range(2)]
    vb3 = [v[:].re("p (a b) -> p a b", b=130) for v in vb]
    PT = [salloc(k, f"pt{i}", 512, BF16) for i in range(4)]
    ksum = [salloc(k, f"ksum{i}", 8) for i in range(2)]
    kmean = [salloc(k, f"kmean{i}", 8) for i in range(2)]
    gsb = salloc(k, "gsb", 128)
    msk = salloc(k, "msk", 64)
    top = [salloc(k, f"top{i}", 8) for i in range(2)]
    biasT = [salloc(k, f"biasT{i}", 1024, BF16) for i in range(2)]
    rc = [salloc(k, f"rc{i}", 8) for i in range(2)]
    on = [salloc(k, f"on{i}", 128) for i in range(2)]
    scale = 128.0 ** -0.5
    for B in range(2):
        fw.memset(vb3[B][:, :, 128:129], 1.0)

    def proj_gen(hd):
        B = hd % 2
        src = k.moba_wqkv[la, hd].rearrange("(kc p) n -> p kc n", p=128)
        sl, wv4 = load_w(k, src, lambda a: a[:, 0:3072].rearrange("p (kc n) -> p kc n", kc=8))
        wv = sl.t[:, 0:3072].rearrange("p (kc s n) -> p kc s n", kc=8, s=3)
        for tg in range(NTG):
            tsl = slice(tg * TG, (tg + 1) * TG)
            pb = bank(k)
            for kc in range(8):
                fw.mm(pb[:], V(wv[:, kc, 1, :], sl), k.hT[:, kc, tsl], start=(kc == 0), stop=(kc == 7))
            fw.copy(kb[B][:, tsl], pb[:], E=fw.act)
            fw.reduce(ksum[B][:, 2 * tg:2 * tg + 2], pb[:].re("p (a b) -> p a b", b=256), ALU.add)
            yield
        fw.ts(kmean[B][:], ksum[B][:], 1.0 / 256.0, None, ALU.mult)
        for tg in range(NTG):
            tsl = slice(tg * TG, (tg + 1) * TG)
            pb = bank(k)
            for kc in range(8):
                fw.mm(pb[:], V(wv[:, kc, 0, :], sl), k.hT[:, kc, tsl], start=(kc == 0), stop=(kc == 7))
            fw.copy(qf[:], pb[:], E=fw.act)
            fw.copy(qb[B][:, tsl], pb[:])
            for t4 in range(4):
                tt_ = tg * 4 + t4
                fw.mm(gps[:, tt_ * 8:(tt_ + 1) * 8], qf[:, t4 * 128:(t4 + 1) * 128], kmean[B][:], start=True, stop=True)
            yield
        fw.tt(gsb[:], gps[:, 0:128], k.negm[:].re("p a b -> p (a b)"), ALU.add)
        for tt_ in range(8, 16):
            tp = top[tt_ % 2]
            o8 = (tt_ - 8) * 8
            fw.max8(tp[:], gsb[:, tt_ * 8:(tt_ + 1) * 8])
            fw.ts(msk[:, o8:o8 + 8], gsb[:, tt_ * 8:(tt_ + 1) * 8], tp[:, 2:3], None, ALU.is_ge)
        fw.ts(msk[:], msk[:], MASKV, -MASKV, ALU.mult, ALU.add)
        yield
        for g in range(2):
            tb = bank(k)
            for t4 in range(4):
                o8 = (g * 4 + t4) * 8
                fw.transpose(tb[0:8, t4 * 128:(t4 + 1) * 128], msk[:, o8:o8 + 8], k.ident[:])
            fw.copy(biasT[B][0:8, g * 512:(g + 1) * 512], tb[0:8, :])
        yield
        for g in range(4):
            pb = bank(k)
            for t4 in range(4):
                tt_ = g * 4 + t4
                for kc in range(8):
                    fw.mm(pb[:, t4 * 128:(t4 + 1) * 128], k.hT[:, kc, tt_ * 128:(tt_ + 1) * 128], V(wv[:, kc, 2, :], sl),
                          start=(kc == 0), stop=(kc == 7))
            fw.copy(vb3[B][:, g * 4:(g + 1) * 4, 0:128], pb[:].re("p (a b) -> p a b", b=128), E=fw.act)
            yield

    def attend(hd, nxt):
        B = hd % 2
        pairs = [(j, i) for j in range(8) for i in [j] + list(range(j))]
        npairs = len(pairs)
        pts = {}

        def st_issue(n):
            j, i = pairs[n]
            qsl = slice(j * 256, (j + 1) * 256)
            sp = bank(k)
            for kt in range(2):
                ksl = slice((2 * i + kt) * 128, (2 * i + kt + 1) * 128)
                osl = slice(kt * 256, (kt + 1) * 256)
                if i == j:
                    fw.mm(sp[:, osl], kb[B][:, ksl], qb[B][:, qsl], start=True, stop=False, inc=False)
                    d0 = kt * 256 + kt * 128
                    fw.mm(sp[:, d0:d0 + 128], k.identb[:], k.tribias[:], start=False, stop=True)
                elif j >= 4:
                    fw.mm(sp[:, osl], kb[B][:, ksl], qb[B][:, qsl], start=True, stop=False, inc=False)
                    fw.mm(sp[:, osl], k.onehot[0:8, i, :], biasT[B][0:8, (j - 4) * 256:(j - 3) * 256], start=False, stop=True)
                else:
                    fw.mm(sp[:, osl], kb[B][:, ksl], qb[B][:, qsl], start=True, stop=True)
            pt = PT[n % 4]
            fw.actf(pt[:], sp[:], AF.Exp, scale=scale)
            pts[n] = pt

        def pv_issue(n):
            j, i = pairs[n]
            pt = pts.pop(n)
            lastpair = (i == j - 1) or (j == 0)
            for qt in range(2):
                ob = obk[qt]
                kts = [0] if (i == j and qt == 0) else [0, 1]
                for n_, kt in enumerate(kts):
                    fin = (n_ == len(kts) - 1)
                    fw.mm(ob[:, 0:129], pt[:, kt * 256 + qt * 128:kt * 256 + (qt + 1) * 128], vb3[B][:, 2 * i + kt, 0:129],
                          start=(i == j and n_ == 0), stop=(lastpair and fin), inc=fin)
            if lastpair:
                for qt in range(2):
                    tt_ = 2 * j + qt
                    ob = obk[qt]
                    fw.recip(rc[qt][:, 0:1], ob[:, 128:129])
                    o_ = on[qt]
                    fw.ts(o_[:], ob[:, 0:128], rc[qt][:, 0:1], None, ALU.mult)
                    tb = bank(k)
                    fw.transpose(tb[:, 0:128], o_[:], k.ident[:])
                    fw.copy(k.big[hd // 4][:, hd % 4, tt_ * 128:(tt_ + 1) * 128], tb[:, 0:128])

        LOOK = 2
        for n in range(min(LOOK, npairs)):
            st_issue(n)
        for n in range(npairs):
            pv_issue(n)
            if n + LOOK < npairs:
                st_issue(n + LOOK)
            if nxt is not None and n % 2 == 1:
                next(nxt, None)
        if nxt is not None:
            for _ in nxt:
                pass

    g0 = proj_gen(0)
    for _ in g0:
        pass
    for hd in range(8):
        nxt = proj_gen(hd + 1) if hd + 1 < 8 else None
        attend(hd, nxt)
    k.nrot = 8
    out_proj(k, k.moba_wo[la], l, s)


def emit_hgrn(k, l, s):
    fw, nc = k.fw, k.nc
    phase_begin(k)
    k.nrot = 2
    obk = k.banks[7]
    kvbanks = [k.banks[2], k.banks[6]]
    qs = salloc(k, "qs", TG)
    kk = salloc(k, "kk", TG)
    t1 = salloc(k, "t1", TG)
    b = salloc(k, "b", TG)
    qtb = [salloc(k, f"qtb{i}", TG, BF16) for i in range(2)]
    ktb = [salloc(k, f"ktb{i}", TG, BF16) for i in range(2)]
    t3 = [salloc(k, f"t3{i}", TG) for i in range(2)]
    t4 = [salloc(k, f"t4{i}", TG) for i in range(2)]
    vsb = [salloc(k, f"vsb{i}", TG, BF16) for i in range(2)]
    gs = [salloc(k, f"gs{i}", TG) for i in range(2)]
    dec = [salloc(k, f"dec{i}", 8) for i in range(2)]
    osb = salloc(k, "osb", TG)
    koT = [salloc(k, f"ko{i}", 128, BF16) for i in range(4)]
    ATm = [salloc(k, f"am{i}", 128, BF16) for i in range(2)]
    st = [salloc(k, f"st{i}", 128) for i in range(4)]
    osq = salloc(k, "osq", TG, BF16)
    rs = salloc(k, "rs", TG)
    hmu = V(k.hmask.t[:, :].bitcast(mybir.dt.uint32), k.hmask)
    cmask = salloc(k, "cmask", TG)
    fw.memset(cmask[:], 1.0)
    fw.memset(cmask[:, 0:TG:64], 0.0)
    fw.memset(ATm[0][:], 0.0)
    fw.memset(ATm[1][:], 0.0)
    wcur = {}
    state = {"cur": 0}

    def s1(it):
        hd, tg = it // 4, it % 4
        B = it % 2
        tsl = slice(tg * TG, (tg + 1) * TG)
        if tg == 0:
            src = k.hgrn_w_in[0, hd].rearrange("(kc p) n -> p kc n", p=128)
            sl, wv4 = load_w(k, src, lambda a: a.rearrange("p (kc n) -> p kc n", kc=8))
            wcur["sl"] = sl
            wcur["wv"] = sl.t[:, :].rearrange("p (kc s n) -> p kc s n", kc=8, s=4)
        sl, wv = wcur["sl"], wcur["wv"]
        pq, pf, pg, pv = k.banks[3], k.banks[4], k.banks[5], k.banks[3]
        for pb, si in ((pq, 0), (pg, 3), (pf, 1)):
            for kc in range(8):
                fw.mm(pb[:], V(wv[:, kc, si, :], sl), k.hT[:, kc, tsl], start=(kc == 0), stop=(kc == 7))
            yield
        fw.actf(qs[:], pq[:], AF.Silu)
        fw.actf(gs[B][:], pg[:], AF.Silu)
        fw.actf(t1[:], pf[:], AF.Tanh, scale=0.5)
        yield
        for t4i in range(4):
            tok = slice(tg * TG + t4i * 128, tg * TG + (t4i + 1) * 128)
            for kc in range(8):
                fw.mm(pv[:, t4i * 128:(t4i + 1) * 128], k.hT[:, kc, tok], V(wv[:, kc, 2, :], sl),
                      start=(kc == 0), stop=(kc == 7))
            yield
        fw.copy(vsb[B][:], pv[:])
        fw.ts(t1[:], t1[:], k.hc[:, 8 + hd:9 + hd], k.hc[:, hd:hd + 1], ALU.mult, ALU.add)
        yield
        fw.ts(kk[:], t1[:], -1.0, 1.0, ALU.mult, ALU.add)
        fw.actf(t1[:], t1[:], AF.Ln)
        yield
        fw.scan(b[:], cmask[:], t1[:], 0.0, ALU.mult, ALU.add)
        b3 = b[:].re("p (n c) -> p n c", c=64)
        bref = b3[:, :, 32:33].bc([128, 8, 64])
        blast = b3[:, :, 63:64].bc([128, 8, 64])
        fw.tt(t1[:].re("p (n c) -> p n c", c=64), b3, bref, ALU.subtract)
        yield
        fw.actf(t4[B][:], t1[:], AF.Exp)
        fw.actf(t1[:], t1[:], AF.Exp, scale=-1.0)
        fw.actf(t3[B][:], b[:], AF.Exp)
        yield
        fw.tt(qtb[B][:], t4[B][:], qs[:], ALU.mult)
        fw.tt(ktb[B][:], t1[:], kk[:], ALU.mult)
        yield
        fw.tt(t4[B][:].re("p (n c) -> p n c", c=64), blast, b3, ALU.subtract)
        fw.actf(t4[B][:], t4[B][:], AF.Exp)
        fw.actf(dec[B][:], b[:, 63:TG:64], AF.Exp)
        yield
        fw.tt(t3[B][:], t3[B][:], qs[:], ALU.mult)
        fw.tt(t4[B][:], t4[B][:], kk[:], ALU.mult)
        yield

    def s2(it):
        hd, tg = it // 4, it % 4
        B = it % 2
        tsl = slice(tg * TG, (tg + 1) * TG)
        if tg == 0:
            fw.memset(st[0][:], 0.0)
            state["cur"] = 0
        base = state["cur"]
        for t4i in range(4):
            tl = slice(t4i * 128, (t4i + 1) * 128)
            tb = bank(k)
            fw.transpose(tb[:, 0:128], t4[B][:, tl], k.ident[:])
            fw.copy(koT[t4i][:], tb[:, 0:128])
        yield

        def kv(n):
            t4i, cc = n // 2, n % 2
            tl = slice(t4i * 128, (t4i + 1) * 128)
            r = t4i * 128
            fw.mm(kvbanks[cc][:, r:r + 128], koT[t4i][cc * 64:(cc + 1) * 64, :], vsb[B][cc * 64:(cc + 1) * 64, tl],
                  start=True, stop=True)

        def upd(n):
            t4i, cc = n // 2, n % 2
            r = t4i * 128
            fw.stt(st[(base + n + 1) % 4][:], st[(base + n) % 4][:], dec[B][:, n:n + 1], kvbanks[cc][:, r:r + 128],
                   ALU.mult, ALU.add)

        def outputs(t4i):
            tl = slice(t4i * 128, (t4i + 1) * 128)
            ab = bank(k)
            fw.mm(ab[:, 0:128], ktb[B][:, tl], qtb[B][:, tl], start=True, stop=True)
            am = ATm[t4i % 2]
            fw.copy_pred(am[:], hmu, ab[:, 0:128])
            ob = obk
            fw.mm(ob[:, 0:128], vsb[B][:, tl], am[:], start=True, stop=False, inc=False)
            for cc in range(2):
                n = t4i * 2 + cc
                csl = slice(t4i * 128 + cc * 64, t4i * 128 + (cc + 1) * 64)
                fw.mm(ob[:, cc * 64:(cc + 1) * 64], st[(base + n) % 4][:], t3[B][:, csl], start=False, stop=(cc == 1), inc=True)
            fw.copy(osb[:, tl], ob[:, 0:128], E=fw.act)

        for n in range(8):
            kv(n)
        yield
        upd(0); upd(1); upd(2)
        yield
        outputs(0)
        yield
        outputs(1)
        yield
        upd(3); upd(4); upd(5); upd(6)
        yield
        outputs(2)
        yield
        outputs(3)
        yield
        upd(7)
        state["cur"] = (base + 8) % 4
        fw.tt(osq[:], osb[:], osb[:], ALU.mult)
        nb_ = bank(k)
        fw.mm(nb_[:], k.onesb[:], osq[:], start=True, stop=True)
        rstd_from_ss(k, nb_[:], rs[:], rs[:], 128, TG)
        yield
        fw.tt(osb[:], osb[:], rs[:], ALU.mult)
        fw.stt(k.big[hd // 4][:, hd % 4, tsl], osb[:], cs(k, C_GN), gs[B][:], ALU.mult, ALU.mult)
        yield

    NIT = 32
    for _ in s1(0):
        pass
    for it in range(NIT):
        g2 = s2(it)
        g1 = s1(it + 1) if it + 1 < NIT else None
        a_live, b_live = True, g1 is not None
        while a_live or b_live:
            if a_live:
                try:
                    next(g2)
                except StopIteration:
                    a_live = False
            if b_live:
                try:
                    next(g1)
                except StopIteration:
                    b_live = False
    k.nrot = 8
    out_proj(k, k.hgrn_wo[0], l, s)


def emit_rglru(k, l, s):
    fw, nc = k.fw, k.nc
    phase_begin(k)
    xbr = salloc(k, "xbr", 2 * 520)
    xbr3 = xbr[:].re("p (a b) -> p a b", b=520)
    xcv = salloc(k, "xcv", 2 * TG)
    xcv3 = xcv[:].re("p (a b) -> p a b", b=TG)
    xcb = salloc(k, "xcb", 2 * TG, BF16)
    xcb3 = xcb[:].re("p (a b) -> p a b", b=TG)
    yg = salloc(k, "yg", 2 * TG)
    yg3 = yg[:].re("p (a b) -> p a b", b=TG)
    r_ = salloc(k, "r", TG)
    gi_ = salloc(k, "gi", TG)
    a_ = salloc(k, "a", TG)
    th_ = salloc(k, "th", TG)
    u_ = salloc(k, "u", TG)
    hs_ = salloc(k, "hs", TG)
    hlast = salloc(k, "hlast", 8)
    for nb in range(4):
        src = k.rg_w_in[0, nb].rearrange("(kc p) n -> p kc n", p=128)
        sl, wv4 = load_w(k, src, lambda a: a.rearrange("p (kc n) -> p kc n", kc=8))
        wv = sl.t[:, :].rearrange("p (kc s n) -> p kc s n", kc=8, s=2)
        sl2 = wslot(k)
        v2 = sl2.t[:, 0:1024].rearrange("p (g kc n) -> p g kc n", g=2, kc=2)
        fw.dma(fw.pool, V(v2[:, 0], sl2), k.rg_w_a[0, nb].rearrange("(kc p) n -> p kc n", p=128), in_dram=True)
        fw.dma(fw.pool, V(v2[:, 1], sl2), k.rg_w_i[0, nb].rearrange("(kc p) n -> p kc n", p=128), in_dram=True)
        fw.memset(xbr3[:, :, 0:3], 0.0)
        for tg in range(NTG):
            tsl = slice(tg * TG, (tg + 1) * TG)
            for ch in range(2):
                c = nb * 2 + ch
                py = bank(k)
                for kc in range(8):
                    fw.mm(py[:], V(wv[:, kc, 0, ch * 128:(ch + 1) * 128], sl), k.hT[:, kc, tsl], start=(kc == 0), stop=(kc == 7))
                fw.actf(yg3[:, ch, :], py[:], AF.Gelu_apprx_tanh)
                px = bank(k)
                for kc in range(8):
                    fw.mm(px[:], V(wv[:, kc, 1, ch * 128:(ch + 1) * 128], sl), k.hT[:, kc, tsl], start=(kc == 0), stop=(kc == 7))
                fw.copy(xbr3[:, ch, 3:515], px[:], E=fw.act)
                fw.ts(xcv3[:, ch, :], xbr3[:, ch, 0:512], cs(k, C_CW + c), cs(k, C_CB + c), ALU.mult, ALU.add)
                for jj in range(1, 4):
                    fw.stt(xcv3[:, ch, :], xbr3[:, ch, jj:jj + 512], cs(k, C_CW + jj * 8 + c), xcv3[:, ch, :], ALU.mult, ALU.add)
                fw.copy(xcb3[:, ch, :], xcv3[:, ch, :])
                fw.copy(xbr3[:, ch, 0:3], xbr3[:, ch, 512:515])
            for e in range(2):
                c = nb * 2 + e
                pr, pi_ = bank(k), bank(k)
                for kc2 in range(2):
                    fw.mm(pr[:], V(v2[:, 0, kc2, e * 128:(e + 1) * 128], sl2), xcb3[:, kc2, :], start=(kc2 == 0), stop=(kc2 == 1))
                for kc2 in range(2):
                    fw.mm(pi_[:], V(v2[:, 1, kc2, e * 128:(e + 1) * 128], sl2), xcb3[:, kc2, :], start=(kc2 == 0), stop=(kc2 == 1))
                fw.actf(r_[:], pr[:], AF.Sigmoid, bias=cs(k, C_BA + c))
                fw.actf(gi_[:], pi_[:], AF.Sigmoid, bias=cs(k, C_BI + c))
                fw.actf(a_[:], r_[:], AF.Exp, scale=k.cl[:, c:c + 1])
                fw.actf(th_[:], r_[:], AF.Tanh, scale=k.ncl[:, c:c + 1])
                fw.tt(r_[:], a_[:], a_[:], ALU.mult)
                fw.stt(r_[:], r_[:], 1.0, th_[:], ALU.add, ALU.mult)
                fw.actf(r_[:], r_[:], AF.Sqrt)
                if tg == 0:
                    fw.memset(r_[:, 0:1], 1.0)
                fw.tt(u_[:], gi_[:], xcv3[:, e, :], ALU.mult)
                fw.tt(u_[:], u_[:], r_[:], ALU.mult)
                init = 0.0 if tg == 0 else hlast[:, c:c + 1]
                fw.scan(hs_[:], a_[:], u_[:], init, ALU.mult, ALU.add)
                fw.copy(hlast[:, c:c + 1], hs_[:, 511:512])
                fw.tt(k.big[c // 4][:, c % 4, tsl], hs_[:], yg3[:, e, :], ALU.mult)
    out_proj(k, k.rg_wo[0], l, s)


def _fm(v):
    return np.ascontiguousarray(np.asarray(v, np.float32).reshape(-1, 128).T)


def _consts(inp, b0):
    cst = np.zeros((128, NCONST), np.float32)
    for s in range(2):
        cst[:, C_COND + s:C_COND + 16:2] = _fm(inp["c"][b0 + s])
    for l in range(4):
        cst[:, C_ADAB + l * 48:C_ADAB + (l + 1) * 48] = _fm(inp["ada_b"][l])
        cst[:, C_NMIX + l * 8:C_NMIX + (l + 1) * 8] = _fm(inp["norm_mix"][l])
        cst[:, C_NMLP + l * 8:C_NMLP + (l + 1) * 8] = _fm(inp["norm_mlp"][l])
        cst[:, C_LB + l * 8:C_LB + (l + 1) * 8] = _fm(inp["hgrn_lb"][l])
    cst[:, C_FIN:C_FIN + 8] = _fm(inp["final_norm"])
    cst[:, C_GN] = np.asarray(inp["hgrn_norm"], np.float32)[0]
    for j in range(4):
        cst[:, C_CW + j * 8:C_CW + (j + 1) * 8] = _fm(inp["rg_conv_w"][0, j])
    cst[:, C_CB:C_CB + 8] = _fm(inp["rg_conv_b"][0])
    cst[:, C_BA:C_BA + 8] = _fm(inp["rg_b_a"][0])
    cst[:, C_BI:C_BI + 8] = _fm(inp["rg_b_i"][0])
    cst[:, C_LAM:C_LAM + 8] = _fm(inp["rg_lambda"][0])
    return cst


WNAMES = ["ada_w", "mlp_up", "mlp_down", "moba_wqkv", "moba_wo", "hgrn_w_in", "hgrn_wo",
          "rg_w_in", "rg_w_a", "rg_w_i", "rg_wo"]


def make_in_maps(inp, n_cores=8):
    x = np.asarray(inp["x"], np.float32)
    shared = {n: np.ascontiguousarray(np.asarray(inp[n], np.float32)) for n in WNAMES}
    w = shared["moba_wqkv"].reshape(2, D, 3, 8, 128).transpose(0, 3, 1, 2, 4)
    shared["moba_wqkv"] = np.ascontiguousarray(w).reshape(2, 8, D, 384)
    w = shared["hgrn_w_in"].reshape(1, D, 4, 8, 128).transpose(0, 3, 1, 2, 4)
    shared["hgrn_w_in"] = np.ascontiguousarray(w).reshape(1, 8, D, 512)
    w = shared["rg_w_in"].reshape(1, D, 2, 4, 256).transpose(0, 3, 1, 2, 4)
    shared["rg_w_in"] = np.ascontiguousarray(w).reshape(1, 4, D, 512)
    maps = []
    for core in range(n_cores):
        b0 = 2 * core
        m = dict(shared)
        m["xT"] = np.ascontiguousarray(x[b0:b0 + 2].transpose(0, 2, 1))
        m["consts"] = _consts(inp, b0)
        maps.append(m)
    return maps


def kernel(**inputs):
    nc = build()
    maps = make_in_maps(inputs)
    res = run_bass_kernel_spmd(nc, maps, core_ids=list(range(8)))
    out = np.empty((16, S, D), np.float32)
    for core in range(8):
        o = np.asarray(res.results[core]["outT"])
        out[2 * core:2 * core + 2] = o.transpose(0, 2, 1)
    return out
```

```python
import numpy as np
import concourse.bass as bass
import concourse.mybir as mybir
from concourse.bass_utils import run_bass_kernel_spmd

F32 = mybir.dt.float32
BF16 = mybir.dt.bfloat16
ALU = mybir.AluOpType
AF = mybir.ActivationFunctionType
AX = mybir.AxisListType

STRICT_SAME_ENGINE = True


class Buf:
    def __init__(self, fw, name, t):
        self.fw = fw
        self.name = name
        self.t = t
        self.last_w = None
        self.reads = {}
        self.dsem = None
        self.dcount = 0
        self.is_psum = False

    def __getitem__(self, idx):
        return V(self.t[idx], self)

    def ap(self, offset, pat):
        return V(bass.AP(self.t, offset, pat), self)


class V:
    def __init__(self, ap, buf):
        self.ap = ap
        self.buf = buf

    def __getitem__(self, idx):
        return V(self.ap[idx], self.buf)

    def re(self, pat, **kw):
        return V(self.ap.rearrange(pat, **kw), self.buf)

    def bc(self, shape):
        return V(self.ap.broadcast_to(list(shape)), self.buf)


class SemObj:
    def __init__(self, h, name):
        self.h = h
        self.name = name


class Eng:
    def __init__(self, fw, name, eng):
        self.fw = fw
        self.name = name
        self.eng = eng
        self.sem = None
        self.count = 0
        self.waited = {}
        self.pend_r = []
        self.pend_w = []
        self.nwait = 0
        self.nins = 0
        self.prog = []

    def wait_ev(self, ev):
        if ev is None:
            return
        s, v = ev
        if s is self.sem and (not STRICT_SAME_ENGINE or self.name == "pe"):
            return
        if self.waited.get(s, 0) >= v:
            return
        eng = self.eng
        self.prog.append(lambda: eng.wait_ge(s.h, v))
        self.waited[s] = v
        self.nwait += 1


class FW:
    def __init__(self, nc, stack):
        self.nc = nc
        self.stack = stack
        self.pe = self._mk("pe", nc.tensor)
        self.act = self._mk("act", nc.scalar)
        self.dve = self._mk("dve", nc.vector)
        self.pool = self._mk("pool", nc.gpsimd)
        self.sp = self._mk("sp", nc.sync)
        self.nsem = 5

    def _mk(self, name, eng):
        e = Eng(self, name, eng)
        e.sem = SemObj(self.stack.enter_context(self.nc.semaphore("s_" + name)), name)
        return e

    def sbuf(self, name, shape, dt):
        t = self.stack.enter_context(self.nc.sbuf_tensor(name, list(shape), dt))
        return Buf(self, name, t)

    def psum(self, name, shape, dt=F32):
        t = self.stack.enter_context(self.nc.psum_tensor(name, list(shape), dt))
        b = Buf(self, name, t)
        b.is_psum = True
        return b

    def op(self, E, fn, outs, ins, inc=True):
        for v in ins:
            E.wait_ev(v.buf.last_w)
            if v.buf.is_psum:
                for s, val in list(v.buf.reads.items()):
                    if s is not E.sem:
                        E.wait_ev((s, val))
        for v in outs:
            E.wait_ev(v.buf.last_w)
            for s, val in list(v.buf.reads.items()):
                E.wait_ev((s, val))
        E.nins += 1
        E.pend_r.extend(v.buf for v in ins)
        E.pend_w.extend(v.buf for v in outs)
        if inc:
            semh = E.sem.h
            E.prog.append(lambda: fn().then_inc(semh, 1))
        else:
            E.prog.append(fn)
        if inc:
            E.count += 1
            ev = (E.sem, E.count)
            for b in E.pend_r:
                if b.reads.get(E.sem, 0) < E.count:
                    b.reads[E.sem] = E.count
            for b in E.pend_w:
                b.last_w = ev
                b.reads = {}
            E.pend_r = []
            E.pend_w = []
        return None

    def dma(self, E, out, in_, out_dram=False, in_dram=False, **kw):
        tracked = None
        if not in_dram:
            E.wait_ev(in_.buf.last_w)
            tracked = in_.buf
        if not out_dram:
            E.wait_ev(out.buf.last_w)
            for s, val in list(out.buf.reads.items()):
                E.wait_ev((s, val))
            tracked = out.buf
        b = tracked
        if b.dsem is None:
            b.dsem = SemObj(self.stack.enter_context(self.nc.semaphore("d_" + b.name)), "d_" + b.name)
            self.nsem += 1
        oap = out if out_dram else out.ap
        iap = in_ if in_dram else in_.ap
        eng = E.eng
        dh = b.dsem.h
        E.prog.append(lambda: eng.dma_start(out=oap, in_=iap, **kw).then_inc(dh, 16))
        b.dcount += 16
        ev = (b.dsem, b.dcount)
        E.nins += 1
        if not out_dram:
            out.buf.last_w = ev
            out.buf.reads = {}
        if not in_dram:
            if in_.buf.reads.get(b.dsem, 0) < b.dcount:
                in_.buf.reads[b.dsem] = b.dcount
        return ev

    def mm(self, out, lhsT, rhs, start=True, stop=True, inc=None):
        if inc is None:
            inc = stop
        return self.op(self.pe, lambda: self.nc.tensor.matmul(out.ap, lhsT.ap, rhs.ap, start=start, stop=stop),
                       [out], [lhsT, rhs], inc=inc)

    def transpose(self, out, in_, ident):
        return self.op(self.pe, lambda: self.nc.tensor.transpose(out.ap, in_.ap, ident.ap), [out], [in_, ident])

    def actf(self, out, in_, func, bias=None, scale=None, accum_out=None, E=None):
        ins = [in_]
        kw = {}
        if bias is not None:
            if isinstance(bias, V):
                ins.append(bias); kw["bias"] = bias.ap
            else:
                kw["bias"] = bias
        if scale is not None:
            if isinstance(scale, V):
                ins.append(scale); kw["scale"] = scale.ap
            else:
                kw["scale"] = scale
        outs = [out]
        if accum_out is not None:
            outs.append(accum_out); kw["accum_out"] = accum_out.ap
        return self.op(self.act, lambda: self.nc.scalar.activation(out.ap, in_.ap, func, **kw), outs, ins)

    def _ve(self, E):
        return E if E is not None else self.dve

    def tt(self, out, in0, in1, op, E=None):
        E = self._ve(E)
        return self.op(E, lambda: E.eng.tensor_tensor(out.ap, in0.ap, in1.ap, op), [out], [in0, in1])

    def ts(self, out, in0, s1, s2, op0, op1=None, E=None, accum_out=None):
        E = self._ve(E)
        ins = [in0]
        a1 = s1
        if isinstance(s1, V):
            ins.append(s1); a1 = s1.ap
        a2 = s2
        if isinstance(s2, V):
            ins.append(s2); a2 = s2.ap
        kw = {}
        outs = [out]
        if op1 is not None:
            kw["op1"] = op1
        if accum_out is not None:
            kw["accum_out"] = accum_out.ap; outs.append(accum_out)
        return self.op(E, lambda: E.eng.tensor_scalar(out.ap, in0.ap, a1, a2, op0, **kw), outs, ins)

    def stt(self, out, in0, scalar, in1, op0, op1):
        ins = [in0, in1]
        a = scalar
        if isinstance(scalar, V):
            ins.append(scalar); a = scalar.ap
        return self.op(self.dve, lambda: self.nc.vector.scalar_tensor_tensor(out.ap, in0.ap, a, in1.ap, op0, op1),
                       [out], ins)

    def copy(self, out, in_, E=None):
        E = self._ve(E)
        if E is self.act:
            return self.op(E, lambda: self.nc.scalar.copy(out.ap, in_.ap), [out], [in_])
        return self.op(E, lambda: E.eng.tensor_copy(out.ap, in_.ap), [out], [in_])

    def memset(self, out, val, E=None):
        E = self._ve(E)
        return self.op(E, lambda: E.eng.memset(out.ap, val), [out], [])

    def reduce(self, out, in_, op, axis=AX.X):
        return self.op(self.dve, lambda: self.nc.vector.tensor_reduce(out.ap, in_.ap, axis, op), [out], [in_])

    def copy_pred(self, out, mask, data):
        return self.op(self.dve, lambda: self.nc.vector.copy_predicated(out.ap, mask.ap, data.ap), [out], [mask, data])

    def recip(self, out, in_):
        return self.op(self.dve, lambda: self.nc.vector.reciprocal(out.ap, in_.ap), [out], [in_])

    def max8(self, out, in_):
        return self.op(self.dve, lambda: self.nc.vector.max(out.ap, in_.ap), [out], [in_])

    def scan(self, out, d0, d1, initial, op0, op1):
        ins = [d0, d1]
        a = initial
        if isinstance(initial, V):
            ins.append(initial); a = initial.ap
        return self.op(self.dve, lambda: self.nc.vector.tensor_tensor_scan(out.ap, d0.ap, d1.ap, a, op0, op1),
                       [out], ins)

    def run(self):
        with self.nc.Block() as block:
            def mk(E):
                def body(_e):
                    for c in E.prog:
                        c()
                return body
            block.tensor(mk(self.pe))
            block.scalar(mk(self.act))
            block.vector(mk(self.dve))
            block.gpsimd(mk(self.pool))
            block.sync(mk(self.sp))

    def finish(self, bufs):
        for b in bufs:
            self.sp.wait_ev(b.last_w)
            for s, val in list(b.reads.items()):
                self.sp.wait_ev((s, val))

import contextlib

D = 1024
S = 2048
TG = 512
NTG = 4
DFF = 4096
EPS = 1e-6
NSLOT = 4

C_COND = 0
C_ADAB = 16
C_NMIX = 208
C_NMLP = 240
C_FIN = 272
C_LB = 280
C_GN = 312
C_CW = 313
C_CB = 345
C_BA = 353
C_BI = 361
C_LAM = 369
NCONST = 384
NEG = -1.0e30
MASKV = 30000.0


class K:
    pass


LAST_FW = None
DEBUG = False


def build(layers=(0, 1, 2, 3), nseq=2, final=True, dbg=None):
    nc = bass.Bass("TRN2", target_bir_lowering=False)
    k = K()
    k.nc = nc
    dt = nc.dram_tensor
    k.xT_d = dt("xT", [2, D, S], F32, kind="ExternalInput").ap()
    k.consts_d = dt("consts", [128, NCONST], F32, kind="ExternalInput").ap()
    k.ada_w = dt("ada_w", [4, D, 6 * D], F32, kind="ExternalInput").ap()
    k.mlp_up = dt("mlp_up", [4, D, DFF], F32, kind="ExternalInput").ap()
    k.mlp_down = dt("mlp_down", [4, DFF, D], F32, kind="ExternalInput").ap()
    k.moba_wqkv = dt("moba_wqkv", [2, 8, D, 384], F32, kind="ExternalInput").ap()
    k.moba_wo = dt("moba_wo", [2, D, D], F32, kind="ExternalInput").ap()
    k.hgrn_w_in = dt("hgrn_w_in", [1, 8, D, 512], F32, kind="ExternalInput").ap()
    k.hgrn_wo = dt("hgrn_wo", [1, D, D], F32, kind="ExternalInput").ap()
    k.rg_w_in = dt("rg_w_in", [1, 4, D, 512], F32, kind="ExternalInput").ap()
    k.rg_w_a = dt("rg_w_a", [1, 4, 256, 256], F32, kind="ExternalInput").ap()
    k.rg_w_i = dt("rg_w_i", [1, 4, 256, 256], F32, kind="ExternalInput").ap()
    k.rg_wo = dt("rg_wo", [1, D, D], F32, kind="ExternalInput").ap()
    k.outT_d = dt("outT", [2, D, S], F32, kind="ExternalOutput").ap()
    k.dbg_d = dt("dbg", [8, 128, 2080], F32, kind="ExternalOutput").ap() if DEBUG else None

    with contextlib.ExitStack() as st:
        fw = FW(nc, st)
        k.fw = fw
        global LAST_FW
        LAST_FW = fw
        k.xT = [fw.sbuf(f"xT{c}", [128, S], F32) for c in range(8)]
        k.hT = fw.sbuf("hT", [128, 8, S], BF16)
        k.big = [fw.sbuf(f"big{i}", [128, 4, S], BF16) for i in range(2)]
        k.slots = [fw.sbuf(f"ws{i}", [128, 4096], BF16) for i in range(NSLOT)]
        k.slot_i = 0
        k.cst = fw.sbuf("cst", [128, NCONST], F32)
        k.condT = fw.sbuf("condT", [128, 8, 2], BF16)
        k.modT = fw.sbuf("modT", [128, 4, 48, 2], F32)
        k.coef = fw.sbuf("coef", [128, 4, 2, 6, 8], F32)
        k.misc = fw.sbuf("misc", [128, 64], F32)
        k.ident = fw.sbuf("ident", [128, 128], F32)
        k.epsb = fw.sbuf("epsb", [128, 1], F32)
        k.onesb = fw.sbuf("onesb", [128, 128], BF16)
        k.trib = fw.sbuf("trib", [128, 128], BF16)
        k.hmask = fw.sbuf("hmask", [128, 128], F32)
        k.cmask = fw.sbuf("cmask", [128, TG], F32)
        k.negm = fw.sbuf("negm", [128, 16, 8], F32)
        k.identb = fw.sbuf("identb", [128, 128], BF16)
        k.tribias = fw.sbuf("tribias", [128, 128], BF16)
        k.scr = fw.sbuf("scr", [128, 9648], F32)
        k.banks = [fw.psum(f"pb{i}", [128, 512], F32) for i in range(8)]
        k.bank_i = 0
        k.nrot = 8
        k.scr_off = 0
        k.ph = 0

        emit_setup(k)
        first = True
        for s in range(nseq):
            load_x(k, s)
            for l in layers:
                on = lambda n: dbg is None or n in dbg
                if s == 0 and first and on("mod"):
                    emit_mod(k, l)
                    first = False
                if on("norm0"):
                    emit_norm(k, l, s, 0)
                kind = l % 3
                if on("mix"):
                    if kind == 0:
                        emit_moba(k, l, s)
                    elif kind == 1:
                        emit_hgrn(k, l, s)
                    else:
                        emit_rglru(k, l, s)
                if s == 0 and on("mod"):
                    nxt = [m for m in layers if m > l]
                    if nxt:
                        emit_mod(k, nxt[0])
                if on("norm1"):
                    emit_norm(k, l, s, 1)
                if on("mlp"):
                    emit_mlp(k, l, s)
            emit_final(k, s, final)
        fw.finish(k.xT)
        fw.run()
    return nc


def bank(k):
    n = k.nrot
    b = k.banks[k.bank_i % n]
    k.bank_i += 1
    return b


def wslot(k):
    b = k.slots[k.slot_i % NSLOT]
    k.slot_i += 1
    return b


def barrier(k):
    fw = k.fw
    engs = [fw.pe, fw.act, fw.dve, fw.pool]
    for E in engs:
        for X in engs:
            if X is not E and X.count > 0:
                E.wait_ev((X.sem, X.count))


def phase_begin(k):
    barrier(k)
    k.scr_off = 0
    k.ph += 1


def salloc(k, name, n, dtype=F32):
    nf = n if dtype == F32 else (n + 1) // 2
    nf = (nf + 7) // 8 * 8
    assert k.scr_off + nf <= 9648, (name, k.scr_off, nf)
    ap = k.scr.t[:, k.scr_off:k.scr_off + nf]
    k.scr_off += nf
    if dtype != F32:
        ap = ap.bitcast(dtype)[:, 0:n]
    else:
        ap = ap[:, 0:n]
    return Buf(k.fw, f"{name}_{k.ph}", ap)


def cs(k, col, n=1):
    return k.cst[:, col:col + n]


def coef(k, l, s, j, c):
    return k.coef[:, l, s, j, c:c + 1]


def emit_setup(k):
    fw, nc = k.fw, k.nc
    fw.dma(fw.sp, k.cst[:], k.consts_d, in_dram=True)
    fw.memset(k.epsb[:], EPS, E=fw.pool)
    fw.memset(k.ident[:], 1.0, E=fw.pool)
    fw.op(fw.pool, lambda: nc.gpsimd.affine_select(k.ident.t[:], k.ident.t[:], [[-1, 128]], ALU.is_equal, 0.0,
                                                   base=0, channel_multiplier=1), [k.ident[:]], [k.ident[:]])
    fw.memset(k.onesb[:], 1.0, E=fw.pool)
    fw.memset(k.hmask[:], 1.0, E=fw.pool)
    fw.op(fw.pool, lambda: nc.gpsimd.affine_select(k.hmask.t[:], k.hmask.t[:], [[1, 128]], ALU.is_ge, 0.0,
                                                   base=0, channel_multiplier=-1), [k.hmask[:]], [k.hmask[:]])
    fw.copy(k.trib[:], k.hmask[:], E=fw.pool)
    fw.ts(k.tribias[:], k.hmask[:], MASKV, -MASKV, ALU.mult, ALU.add)
    fw.copy(k.identb[:], k.ident[:])
    fw.memset(k.hmask[0:64, 64:128], 0.0, E=fw.pool)
    fw.memset(k.cmask[:], 1.0, E=fw.pool)
    fw.memset(k.cmask[:, 0:TG:64], 0.0, E=fw.pool)
    fw.memset(k.negm[:], 0.0, E=fw.pool)
    for tt in range(16):
        j = tt // 2
        fw.memset(k.negm[:, tt, j:8], NEG, E=fw.pool)
    fw.actf(k.condT[:].re("p a b -> p (a b)"), cs(k, C_COND, 16), AF.Silu)
    m = k.misc
    lbv = k.cst[:, C_LB:C_LB + 32]
    e = k.misc[:, 0:32]
    fw.actf(e, lbv, AF.Exp)
    ssum = k.misc[:, 32:40]
    fw.tt(ssum, k.misc[:, 0:8], k.misc[:, 8:16], ALU.add)
    fw.tt(ssum, ssum, k.misc[:, 16:24], ALU.add)
    fw.tt(ssum, ssum, k.misc[:, 24:32], ALU.add)
    fw.op(fw.dve, lambda: nc.vector.reciprocal(k.misc.t[:, 32:40], k.misc.t[:, 32:40]), [ssum], [ssum])
    k.lb1 = k.misc[:, 40:48]
    k.oml1 = k.misc[:, 48:56]
    fw.tt(k.lb1, k.misc[:, 8:16], ssum, ALU.mult)
    fw.ts(k.oml1, k.lb1, -1.0, 1.0, ALU.mult, ALU.add)
    k.hc = fw.sbuf("hc", [128, 16], F32)
    fw.ts(k.hc[:, 8:16], k.oml1, 0.5, None, ALU.mult)
    fw.ts(k.hc[:, 0:8], k.hc[:, 8:16], -1.0, 1.0, ALU.mult, ALU.add)
    k.cl = k.misc[:, 56:64]
    tmp = k.misc[:, 0:8]
    fw.actf(tmp, cs(k, C_LAM, 8), AF.Exp, scale=-1.0)
    fw.ts(tmp, tmp, 1.0, None, ALU.add)
    fw.actf(tmp, tmp, AF.Ln)
    fw.ts(k.cl, tmp, -8.0, None, ALU.mult)
    k.ncl = fw.sbuf("ncl", [128, 8], F32)
    fw.ts(k.ncl[:], tmp, 8.0, None, ALU.mult)


def load_x(k, s):
    fw = k.fw
    src = k.xT_d[s].rearrange("(c p) t -> c p t", p=128)
    for c in range(8):
        fw.dma(fw.sp, k.xT[c][:], src[c], in_dram=True)


def emit_mod(k, l):
    fw, nc = k.fw, k.nc
    pb = bank(k)
    for nb in range(12):
        sl, wv = load_w(k, k.ada_w[l][:, nb * 512:(nb + 1) * 512].rearrange("(kc p) n -> p kc n", p=128),
                        lambda a: a.rearrange("p (kc n) -> p kc n", kc=8))
        for n4 in range(4):
            n = nb * 4 + n4
            for kc in range(8):
                fw.mm(pb[:, 2 * n:2 * n + 2], V(wv[:, kc, n4 * 128:(n4 + 1) * 128], sl), k.condT[:, kc, :],
                      start=(kc == 0), stop=(kc == 7))
    for s in range(2):
        src = pb[:, 0:96].ap.rearrange("p (n s) -> p n s", s=2)[:, :, s]
        fw.tt(k.modT[:, l, :, s], V(src, pb), cs(k, C_ADAB + l * 48, 48), ALU.add)
    for s in range(2):
        md = lambda j: k.modT[:, l, j * 8:(j + 1) * 8, s]
        fw.stt(k.coef[:, l, s, 0, :], md(1), 1.0, cs(k, C_NMIX + l * 8, 8), ALU.add, ALU.mult)
        fw.copy(k.coef[:, l, s, 1, :], md(0))
        fw.copy(k.coef[:, l, s, 2, :], md(2))
        fw.stt(k.coef[:, l, s, 3, :], md(4), 1.0, cs(k, C_NMLP + l * 8, 8), ALU.add, ALU.mult)
        fw.copy(k.coef[:, l, s, 4, :], md(3))
        fw.copy(k.coef[:, l, s, 5, :], md(5))


def rstd_from_ss(k, ss_ps, rt, rs, n, width):
    fw, nc = k.fw, k.nc
    fw.actf(rt, ss_ps, AF.Ln, bias=k.epsb[:, 0:1], scale=1.0 / n)
    fw.actf(rs, rt, AF.Exp, scale=-0.5)


def emit_norm(k, l, s, which):
    fw, nc = k.fw, k.nc
    phase_begin(k)
    sq = [salloc(k, f"sq{i}", TG, BF16) for i in range(8)]
    rt = salloc(k, "rt", TG)
    rs = [salloc(k, f"rs{i}", TG) for i in range(2)]
    tmp = [salloc(k, f"tmp{i}", TG) for i in range(4)]
    ja, jb = (0, 1) if which == 0 else (3, 4)
    for tg in range(NTG):
        tsl = slice(tg * TG, (tg + 1) * TG)
        pb = bank(k)
        for c in range(8):
            q = sq[c]
            fw.tt(q[:], k.xT[c][:, tsl], k.xT[c][:, tsl], ALU.mult)
            fw.mm(pb[:], k.onesb[:], q[:], start=(c == 0), stop=(c == 7))
        r = rs[tg % 2]
        rstd_from_ss(k, pb[:], rt[:], r[:], D, TG)
        for c in range(8):
            t = tmp[c % 4]
            fw.stt(t[:], k.xT[c][:, tsl], coef(k, l, s, ja, c), r[:], ALU.mult, ALU.mult)
            fw.actf(k.hT[:, c, tsl], t[:], AF.Identity, bias=coef(k, l, s, jb, c))


def emit_final(k, s, final):
    fw, nc = k.fw, k.nc
    phase_begin(k)
    if final:
        sq = [salloc(k, f"sq{i}", TG, BF16) for i in range(8)]
        rt = salloc(k, "rt", TG)
        rs = [salloc(k, f"rs{i}", TG) for i in range(2)]
        for tg in range(NTG):
            tsl = slice(tg * TG, (tg + 1) * TG)
            pb = bank(k)
            for c in range(8):
                q = sq[c]
                fw.tt(q[:], k.xT[c][:, tsl], k.xT[c][:, tsl], ALU.mult)
                fw.mm(pb[:], k.onesb[:], q[:], start=(c == 0), stop=(c == 7))
            r = rs[tg % 2]
            rstd_from_ss(k, pb[:], rt[:], r[:], D, TG)
            for c in range(8):
                fw.stt(k.xT[c][:, tsl], k.xT[c][:, tsl], cs(k, C_FIN + c), r[:], ALU.mult, ALU.mult)
    dst = k.outT_d[s].rearrange("(c p) t -> c p t", p=128)
    for c in range(8):
        fw.dma(fw.sp, dst[c], k.xT[c][:], out_dram=True)


def load_w(k, src_ap, view):
    fw = k.fw
    sl = wslot(k)
    v = view(sl.t[:, :])
    fw.dma(fw.pool, V(v, sl), src_ap, in_dram=True)
    return sl, v


def out_proj(k, wo_ap, l, s):
    fw = k.fw
    for half in range(2):
        sl, wv = load_w(k, wo_ap[:, half * 512:(half + 1) * 512].rearrange("(h p) n -> p h n", p=128),
                        lambda a: a.rearrange("p (h n) -> p h n", h=8))
        for dc4 in range(4):
            dc = half * 4 + dc4
            for tg in range(NTG):
                tsl = slice(tg * TG, (tg + 1) * TG)
                pb = bank(k)
                for h in range(8):
                    fw.mm(pb[:], V(wv[:, h, dc4 * 128:(dc4 + 1) * 128], sl), k.big[h // 4][:, h % 4, tsl],
                          start=(h == 0), stop=(h == 7))
                fw.stt(k.xT[dc][:, tsl], pb[:], coef(k, l, s, 2, dc), k.xT[dc][:, tsl], ALU.mult, ALU.add)


def emit_mlp(k, l, s):
    fw, nc = k.fw, k.nc
    phase_begin(k)
    rl = [salloc(k, f"rl{i}", TG) for i in range(4)]
    ri = 0

    def load(fb):
        a = load_w(k, k.mlp_up[l][:, fb * 512:(fb + 1) * 512].rearrange("(kc p) n -> p kc n", p=128),
                   lambda a: a.rearrange("p (kc n) -> p kc n", kc=8))
        b = load_w(k, k.mlp_down[l][fb * 512:(fb + 1) * 512, :].rearrange("(kc p) n -> p kc n", p=128),
                   lambda a: a.rearrange("p (kc n) -> p kc n", kc=4))
        return a, b

    def up(fb, wu):
        nonlocal ri
        sl, wv = wu
        u = k.big[fb % 2]
        for fc in range(4):
            for tg in range(NTG):
                tsl = slice(tg * TG, (tg + 1) * TG)
                pb = bank(k)
                for kc in range(8):
                    fw.mm(pb[:], V(wv[:, kc, fc * 128:(fc + 1) * 128], sl), k.hT[:, kc, tsl],
                          start=(kc == 0), stop=(kc == 7))
                r = rl[ri % 4]
                ri += 1
                fw.actf(r[:], pb[:], AF.Relu)
                fw.tt(u[:, fc, tsl], r[:], r[:], ALU.mult)

    def down(fb, wd):
        sl, wv = wd
        u = k.big[fb % 2]
        for dc in range(8):
            for tg in range(NTG):
                tsl = slice(tg * TG, (tg + 1) * TG)
                pb = bank(k)
                for fc in range(4):
                    fw.mm(pb[:], V(wv[:, fc, dc * 128:(dc + 1) * 128], sl), u[:, fc, tsl],
                          start=(fc == 0), stop=(fc == 3))
                fw.stt(k.xT[dc][:, tsl], pb[:], coef(k, l, s, 5, dc), k.xT[dc][:, tsl], ALU.mult, ALU.add)

    ws = {}
    ws[0] = load(0)
    up(0, ws[0][0])
    for fb in range(8):
        if fb + 1 < 8:
            ws[fb + 1] = load(fb + 1)
            up(fb + 1, ws[fb + 1][0])
        down(fb, ws[fb][1])


MOBA_STAGE = 99


def dump(k, i, view, n):
    if DEBUG:
        k.fw.dma(k.fw.pool, k.dbg_d[i][:, 0:n], view, out_dram=True)


SKIP = set()
MOBA_HEADS = 8


def emit_moba(k, l, s):
    fw, nc = k.fw, k.nc
    la = l // 3
    phase_begin(k)
    k.nrot = 7
    gps = k.banks[7]
    qf = [salloc(k, f"qf{i}", TG) for i in range(2)]
    qb = [salloc(k, f"qb{i}", S, BF16) for i in range(2)]
    kb = [salloc(k, f"kb{i}", S, BF16) for i in range(2)]
    vb = [salloc(k, f"vb{i}", 16 * 130, BF16) for i in range(2)]
    vb3 = [v[:].re("p (a b) -> p a b", b=130) for v in vb]
    PT = [salloc(k, f"pt{i}", 512, BF16) for i in range(4)]
    ksum = [salloc(k, f"ksum{i}", 8) for i in range(2)]
    kmean = [salloc(k, f"kmean{i}", 8) for i in range(2)]
    gsb = [salloc(k, f"gsb{i}", 128) for i in range(2)]
    msk = [salloc(k, f"msk{i}", 128) for i in range(2)]
    top = [salloc(k, f"top{i}", 8) for i in range(2)]
    acc = [salloc(k, f"acc{i}", 136) for i in range(2)]
    rc = [salloc(k, f"rc{i}", 8) for i in range(2)]
    on = [salloc(k, f"on{i}", 128) for i in range(2)]
    scale = 128.0 ** -0.5
    for B in range(2):
        fw.memset(vb3[B][:, :, 128:129], 1.0)

    def proj_gen(hd):
        B = hd % 2
        src = k.moba_wqkv[la, hd].rearrange("(kc p) n -> p kc n", p=128)
        sl, wv4 = load_w(k, src, lambda a: a[:, 0:3072].rearrange("p (kc n) -> p kc n", kc=8))
        wv = sl.t[:, 0:3072].rearrange("p (kc s n) -> p kc s n", kc=8, s=3)
        for tg in range(NTG):
            tsl = slice(tg * TG, (tg + 1) * TG)
            pb = bank(k)
            for kc in range(8):
                fw.mm(pb[:], V(wv[:, kc, 1, :], sl), k.hT[:, kc, tsl], start=(kc == 0), stop=(kc == 7))
            fw.copy(kb[B][:, tsl], pb[:], E=fw.act)
            fw.reduce(ksum[B][:, 2 * tg:2 * tg + 2], pb[:].re("p (a b) -> p a b", b=256), ALU.add)
            yield
        fw.ts(kmean[B][:], ksum[B][:], 1.0 / 256.0, None, ALU.mult)
        for tg in range(NTG):
            tsl = slice(tg * TG, (tg + 1) * TG)
            pb = bank(k)
            for kc in range(8):
                fw.mm(pb[:], V(wv[:, kc, 0, :], sl), k.hT[:, kc, tsl], start=(kc == 0), stop=(kc == 7))
            qfb = qf[tg % 2]
            fw.copy(qfb[:], pb[:], E=fw.act)
            fw.copy(qb[B][:, tsl], pb[:], E=fw.act)
            for t4 in range(4):
                tt_ = tg * 4 + t4
                fw.mm(gps[:, tt_ * 8:(tt_ + 1) * 8], qfb[:, t4 * 128:(t4 + 1) * 128], kmean[B][:], start=True, stop=True)
            yield
        fw.tt(gsb[B][:], gps[:, 0:128], k.negm[:].re("p a b -> p (a b)"), ALU.add)
        fw.memset(msk[B][:], 1.0)
        for tt_ in range(8, 16):
            tp = top[tt_ % 2]
            fw.max8(tp[:], gsb[B][:, tt_ * 8:(tt_ + 1) * 8])
            fw.ts(msk[B][:, tt_ * 8:(tt_ + 1) * 8], gsb[B][:, tt_ * 8:(tt_ + 1) * 8], tp[:, 2:3], None, ALU.is_ge)
        yield
        for g in range(4):
            pb = bank(k)
            for t4 in range(4):
                tt_ = g * 4 + t4
                for kc in range(8):
                    fw.mm(pb[:, t4 * 128:(t4 + 1) * 128], k.hT[:, kc, tt_ * 128:(tt_ + 1) * 128], V(wv[:, kc, 2, :], sl),
                          start=(kc == 0), stop=(kc == 7))
            fw.copy(vb3[B][:, g * 4:(g + 1) * 4, 0:128], pb[:].re("p (a b) -> p a b", b=128), E=fw.act)
            yield

    def attend(hd, nxt):
        B = hd % 2
        pairs = [(j, i) for j in range(8) for i in [j] + list(range(j))]
        npairs = len(pairs)
        pts = {}

        def st_issue(n):
            j, i = pairs[n]
            qsl = slice(j * 256, (j + 1) * 256)
            sp = bank(k)
            for kt in range(2):
                ksl = slice((2 * i + kt) * 128, (2 * i + kt + 1) * 128)
                osl = slice(kt * 256, (kt + 1) * 256)
                if i == j:
                    fw.mm(sp[:, osl], kb[B][:, ksl], qb[B][:, qsl], start=True, stop=False, inc=False)
                    d0 = kt * 256 + kt * 128
                    fw.mm(sp[:, d0:d0 + 128], k.identb[:], k.tribias[:], start=False, stop=True)
                else:
                    fw.mm(sp[:, osl], kb[B][:, ksl], qb[B][:, qsl], start=True, stop=True)
            pt = PT[n % 4]
            fw.actf(pt[:], sp[:], AF.Exp, scale=scale)
            pts[n] = pt

        def pv_issue(n):
            j, i = pairs[n]
            pt = pts.pop(n)
            for qt in range(2):
                tt_ = 2 * j + qt
                ob = bank(k)
                kts = [0] if (i == j and qt == 0) else [0, 1]
                for n_, kt in enumerate(kts):
                    fw.mm(ob[:, 0:129], pt[:, kt * 256 + qt * 128:kt * 256 + (qt + 1) * 128], vb3[B][:, 2 * i + kt, 0:129],
                          start=(n_ == 0), stop=(n_ == len(kts) - 1))
                a = acc[qt]
                if i == j:
                    fw.copy(a[:, 0:129], ob[:, 0:129], E=fw.act)
                else:
                    fw.stt(a[:, 0:129], ob[:, 0:129], msk[B][:, tt_ * 8 + i:tt_ * 8 + i + 1], a[:, 0:129], ALU.mult, ALU.add)
            if (i == j - 1) or (j == 0):
                for qt in range(2):
                    tt_ = 2 * j + qt
                    a = acc[qt]
                    fw.recip(rc[qt][:, 0:1], a[:, 128:129])
                    o_ = on[qt]
                    fw.ts(o_[:], a[:, 0:128], rc[qt][:, 0:1], None, ALU.mult)
                    tb = bank(k)
                    fw.transpose(tb[:, 0:128], o_[:], k.ident[:])
                    fw.copy(k.big[hd // 4][:, hd % 4, tt_ * 128:(tt_ + 1) * 128], tb[:, 0:128])

        LOOK = 2
        for n in range(min(LOOK, npairs)):
            st_issue(n)
        for n in range(npairs):
            pv_issue(n)
            if n + LOOK < npairs:
                st_issue(n + LOOK)
            if nxt is not None and n % 2 == 1:
                next(nxt, None)
        if nxt is not None:
            for _ in nxt:
                pass

    g0 = proj_gen(0)
    for _ in g0:
        pass
    for hd in range(8):
        nxt = proj_gen(hd + 1) if hd + 1 < 8 else None
        attend(hd, nxt)
    k.nrot = 8
    out_proj(k, k.moba_wo[la], l, s)


def emit_hgrn(k, l, s):
    fw, nc = k.fw, k.nc
    phase_begin(k)
    k.nrot = 3
    obk = k.banks[7]
    qs = salloc(k, "qs", TG)
    kk = salloc(k, "kk", TG)
    t1 = salloc(k, "t1", TG)
    b = salloc(k, "b", TG)
    t2 = salloc(k, "t2", TG)
    qtb = [salloc(k, f"qtb{i}", TG, BF16) for i in range(2)]
    ktb = [salloc(k, f"ktb{i}", TG, BF16) for i in range(2)]
    t3 = [salloc(k, f"t3{i}", TG) for i in range(2)]
    t4 = [salloc(k, f"t4{i}", TG) for i in range(2)]
    vsb = [salloc(k, f"vsb{i}", TG, BF16) for i in range(2)]
    gs = [salloc(k, f"gs{i}", TG) for i in range(2)]
    dec = [salloc(k, f"dec{i}", 8) for i in range(2)]
    osb = salloc(k, "osb", TG)
    koT = [salloc(k, f"ko{i}", 128, BF16) for i in range(2)]
    ATm = [salloc(k, f"am{i}", 128, BF16) for i in range(2)]
    st = [salloc(k, f"st{i}", 128) for i in range(2)]
    osq = salloc(k, "osq", TG, BF16)
    rs = salloc(k, "rs", TG)
    hmu = V(k.hmask.t[:, :].bitcast(mybir.dt.uint32), k.hmask)
    cmask = k.cmask
    fw.memset(ATm[0][:], 0.0)
    fw.memset(ATm[1][:], 0.0)
    wcur = {}
    state = {"cur": 0}

    def s1(it):
        hd, tg = it // 4, it % 4
        B = it % 2
        tsl = slice(tg * TG, (tg + 1) * TG)
        if tg == 0:
            src = k.hgrn_w_in[0, hd].rearrange("(kc p) n -> p kc n", p=128)
            sl, wv4 = load_w(k, src, lambda a: a.rearrange("p (kc n) -> p kc n", kc=8))
            wcur["sl"] = sl
            wcur["wv"] = sl.t[:, :].rearrange("p (kc s n) -> p kc s n", kc=8, s=4)
        sl, wv = wcur["sl"], wcur["wv"]
        pq, pf, pg, pv = k.banks[3], k.banks[4], k.banks[5], k.banks[6]
        for pb, si in ((pq, 0), (pg, 3), (pf, 1)):
            for kc in range(8):
                fw.mm(pb[:], V(wv[:, kc, si, :], sl), k.hT[:, kc, tsl], start=(kc == 0), stop=(kc == 7))
            yield
        fw.actf(qs[:], pq[:], AF.Silu)
        fw.actf(gs[B][:], pg[:], AF.Silu)
        fw.actf(t1[:], pf[:], AF.Tanh, scale=0.5)
        yield
        for t4i in range(4):
            tok = slice(tg * TG + t4i * 128, tg * TG + (t4i + 1) * 128)
            for kc in range(8):
                fw.mm(pv[:, t4i * 128:(t4i + 1) * 128], k.hT[:, kc, tok], V(wv[:, kc, 2, :], sl),
                      start=(kc == 0), stop=(kc == 7))
            yield
        fw.copy(vsb[B][:], pv[:])
        fw.ts(t1[:], t1[:], k.hc[:, 8 + hd:9 + hd], k.hc[:, hd:hd + 1], ALU.mult, ALU.add)
        yield
        fw.ts(kk[:], t1[:], -1.0, 1.0, ALU.mult, ALU.add)
        fw.actf(t1[:], t1[:], AF.Ln)
        yield
        fw.scan(b[:], cmask[:], t1[:], 0.0, ALU.mult, ALU.add)
        b3 = b[:].re("p (n c) -> p n c", c=64)
        bref = b3[:, :, 32:33].bc([128, 8, 64])
        blast = b3[:, :, 63:64].bc([128, 8, 64])
        fw.tt(t1[:].re("p (n c) -> p n c", c=64), b3, bref, ALU.subtract)
        yield
        fw.actf(t2[:], t1[:], AF.Exp)
        fw.actf(t1[:], t1[:], AF.Exp, scale=-1.0)
        fw.actf(t3[B][:], b[:], AF.Exp)
        yield
        fw.tt(t4[B][:].re("p (n c) -> p n c", c=64), blast, b3, ALU.subtract)
        fw.actf(t4[B][:], t4[B][:], AF.Exp)
        fw.actf(dec[B][:], b[:, 63:TG:64], AF.Exp)
        yield
        fw.tt(qtb[B][:], t2[:], qs[:], ALU.mult)
        fw.tt(ktb[B][:], t1[:], kk[:], ALU.mult)
        yield
        fw.tt(t3[B][:], t3[B][:], qs[:], ALU.mult)
        fw.tt(t4[B][:], t4[B][:], kk[:], ALU.mult)
        yield

    def s2(it):
        hd, tg = it // 4, it % 4
        B = it % 2
        tsl = slice(tg * TG, (tg + 1) * TG)
        if tg == 0:
            fw.memset(st[0][:], 0.0)
            state["cur"] = 0
        for t4i in range(4):
            cur = state["cur"]
            tl = slice(t4i * 128, (t4i + 1) * 128)
            tb = bank(k)
            fw.transpose(tb[:, 0:128], t4[B][:, tl], k.ident[:])
            ko = koT[t4i % 2]
            fw.copy(ko[:], tb[:, 0:128])
            ab = bank(k)
            fw.mm(ab[:, 0:128], ktb[B][:, tl], qtb[B][:, tl], start=True, stop=True)
            am = ATm[t4i % 2]
            fw.copy_pred(am[:], hmu, ab[:, 0:128])
            yield
            ob = obk
            fw.mm(ob[:, 0:128], vsb[B][:, tl], am[:], start=True, stop=False, inc=False)
            for cc in range(2):
                n = t4i * 2 + cc
                csl = slice(t4i * 128 + cc * 64, t4i * 128 + (cc + 1) * 64)
                fw.mm(ob[:, cc * 64:(cc + 1) * 64], st[cur][:], t3[B][:, csl], start=False, stop=(cc == 1), inc=True)
                kvb = bank(k)
                fw.mm(kvb[:, 0:128], ko[cc * 64:(cc + 1) * 64, :], vsb[B][cc * 64:(cc + 1) * 64, tl], start=True, stop=True)
                fw.stt(st[1 - cur][:], st[cur][:], dec[B][:, n:n + 1], kvb[:, 0:128], ALU.mult, ALU.add)
                cur = 1 - cur
            state["cur"] = cur
            fw.copy(osb[:, tl], ob[:, 0:128], E=fw.act)
            yield
        fw.tt(osq[:], osb[:], osb[:], ALU.mult)
        nb_ = bank(k)
        fw.mm(nb_[:], k.onesb[:], osq[:], start=True, stop=True)
        rstd_from_ss(k, nb_[:], rs[:], rs[:], 128, TG)
        yield
        fw.tt(osb[:], osb[:], rs[:], ALU.mult)
        fw.stt(k.big[hd // 4][:, hd % 4, tsl], osb[:], cs(k, C_GN), gs[B][:], ALU.mult, ALU.mult)
        yield

    NIT = 32
    for _ in s1(0):
        pass
    for it in range(NIT):
        g2 = s2(it)
        g1 = s1(it + 1) if it + 1 < NIT else None
        a_live, b_live = True, g1 is not None
        while a_live or b_live:
            if a_live:
                try:
                    next(g2)
                except StopIteration:
                    a_live = False
            if b_live:
                try:
                    next(g1)
                except StopIteration:
                    b_live = False
    k.nrot = 8
    out_proj(k, k.hgrn_wo[0], l, s)


def emit_rglru(k, l, s):
    fw, nc = k.fw, k.nc
    phase_begin(k)
    xbr = salloc(k, "xbr", 2 * 520)
    xbr3 = xbr[:].re("p (a b) -> p a b", b=520)
    xcv = salloc(k, "xcv", 2 * TG)
    xcv3 = xcv[:].re("p (a b) -> p a b", b=TG)
    xcb = salloc(k, "xcb", 2 * TG, BF16)
    xcb3 = xcb[:].re("p (a b) -> p a b", b=TG)
    yg = salloc(k, "yg", 2 * TG)
    yg3 = yg[:].re("p (a b) -> p a b", b=TG)
    r2 = [salloc(k, f"r{i}", TG) for i in range(2)]
    gi2 = [salloc(k, f"gi{i}", TG) for i in range(2)]
    a2_ = [salloc(k, f"a{i}", TG) for i in range(2)]
    th2 = [salloc(k, f"th{i}", TG) for i in range(2)]
    hlast = salloc(k, "hlast", 8)
    for nb in range(4):
        src = k.rg_w_in[0, nb].rearrange("(kc p) n -> p kc n", p=128)
        sl, wv4 = load_w(k, src, lambda a: a.rearrange("p (kc n) -> p kc n", kc=8))
        wv = sl.t[:, :].rearrange("p (kc s n) -> p kc s n", kc=8, s=2)
        sl2 = wslot(k)
        v2 = sl2.t[:, 0:1024].rearrange("p (g kc n) -> p g kc n", g=2, kc=2)
        fw.dma(fw.pool, V(v2[:, 0], sl2), k.rg_w_a[0, nb].rearrange("(kc p) n -> p kc n", p=128), in_dram=True)
        fw.dma(fw.pool, V(v2[:, 1], sl2), k.rg_w_i[0, nb].rearrange("(kc p) n -> p kc n", p=128), in_dram=True)
        fw.memset(xbr3[:, :, 0:3], 0.0)
        for tg in range(NTG):
            tsl = slice(tg * TG, (tg + 1) * TG)
            for ch in range(2):
                c = nb * 2 + ch
                py = bank(k)
                for kc in range(8):
                    fw.mm(py[:], V(wv[:, kc, 0, ch * 128:(ch + 1) * 128], sl), k.hT[:, kc, tsl], start=(kc == 0), stop=(kc == 7))
                fw.actf(yg3[:, ch, :], py[:], AF.Gelu_apprx_tanh)
                px = bank(k)
                for kc in range(8):
                    fw.mm(px[:], V(wv[:, kc, 1, ch * 128:(ch + 1) * 128], sl), k.hT[:, kc, tsl], start=(kc == 0), stop=(kc == 7))
                fw.copy(xbr3[:, ch, 3:515], px[:], E=fw.act)
                fw.ts(xcv3[:, ch, :], xbr3[:, ch, 0:512], cs(k, C_CW + c), cs(k, C_CB + c), ALU.mult, ALU.add)
                for jj in range(1, 4):
                    fw.stt(xcv3[:, ch, :], xbr3[:, ch, jj:jj + 512], cs(k, C_CW + jj * 8 + c), xcv3[:, ch, :], ALU.mult, ALU.add)
                fw.copy(xcb3[:, ch, :], xcv3[:, ch, :])
                fw.copy(xbr3[:, ch, 0:3], xbr3[:, ch, 512:515])
            for e in range(2):
                c = nb * 2 + e
                pr, pi_ = bank(k), bank(k)
                for kc2 in range(2):
                    fw.mm(pr[:], V(v2[:, 0, kc2, e * 128:(e + 1) * 128], sl2), xcb3[:, kc2, :], start=(kc2 == 0), stop=(kc2 == 1))
                for kc2 in range(2):
                    fw.mm(pi_[:], V(v2[:, 1, kc2, e * 128:(e + 1) * 128], sl2), xcb3[:, kc2, :], start=(kc2 == 0), stop=(kc2 == 1))
                fw.actf(r2[e][:], pr[:], AF.Sigmoid, bias=cs(k, C_BA + c))
                fw.actf(gi2[e][:], pi_[:], AF.Sigmoid, bias=cs(k, C_BI + c))
            for e in range(2):
                c = nb * 2 + e
                fw.actf(a2_[e][:], r2[e][:], AF.Exp, scale=k.cl[:, c:c + 1])
                fw.actf(th2[e][:], r2[e][:], AF.Tanh, scale=k.ncl[:, c:c + 1])
                fw.tt(r2[e][:], a2_[e][:], a2_[e][:], ALU.mult)
                fw.stt(r2[e][:], r2[e][:], 1.0, th2[e][:], ALU.add, ALU.mult)
                fw.tt(gi2[e][:], gi2[e][:], xcv3[:, e, :], ALU.mult)
            for e in range(2):
                c = nb * 2 + e
                fw.actf(r2[e][:], r2[e][:], AF.Sqrt)
                if tg == 0:
                    fw.memset(r2[e][:, 0:1], 1.0)
                fw.tt(gi2[e][:], gi2[e][:], r2[e][:], ALU.mult)
                init = 0.0 if tg == 0 else hlast[:, c:c + 1]
                fw.scan(th2[e][:], a2_[e][:], gi2[e][:], init, ALU.mult, ALU.add)
                fw.copy(hlast[:, c:c + 1], th2[e][:, 511:512])
                fw.tt(k.big[c // 4][:, c % 4, tsl], th2[e][:], yg3[:, e, :], ALU.mult)
    out_proj(k, k.rg_wo[0], l, s)


def _fm(v):
    return np.ascontiguousarray(np.asarray(v, np.float32).reshape(-1, 128).T)


def _consts(inp, b0):
    cst = np.zeros((128, NCONST), np.float32)
    for s in range(2):
        cst[:, C_COND + s:C_COND + 16:2] = _fm(inp["c"][b0 + s])
    for l in range(4):
        cst[:, C_ADAB + l * 48:C_ADAB + (l + 1) * 48] = _fm(inp["ada_b"][l])
        cst[:, C_NMIX + l * 8:C_NMIX + (l + 1) * 8] = _fm(inp["norm_mix"][l])
        cst[:, C_NMLP + l * 8:C_NMLP + (l + 1) * 8] = _fm(inp["norm_mlp"][l])
        cst[:, C_LB + l * 8:C_LB + (l + 1) * 8] = _fm(inp["hgrn_lb"][l])
    cst[:, C_FIN:C_FIN + 8] = _fm(inp["final_norm"])
    cst[:, C_GN] = np.asarray(inp["hgrn_norm"], np.float32)[0]
    for j in range(4):
        cst[:, C_CW + j * 8:C_CW + (j + 1) * 8] = _fm(inp["rg_conv_w"][0, j])
    cst[:, C_CB:C_CB + 8] = _fm(inp["rg_conv_b"][0])
    cst[:, C_BA:C_BA + 8] = _fm(inp["rg_b_a"][0])
    cst[:, C_BI:C_BI + 8] = _fm(inp["rg_b_i"][0])
    cst[:, C_LAM:C_LAM + 8] = _fm(inp["rg_lambda"][0])
    return cst


WNAMES = ["ada_w", "mlp_up", "mlp_down", "moba_wqkv", "moba_wo", "hgrn_w_in", "hgrn_wo",
          "rg_w_in", "rg_w_a", "rg_w_i", "rg_wo"]


def make_in_maps(inp, n_cores=8):
    x = np.asarray(inp["x"], np.float32)
    shared = {n: np.ascontiguousarray(np.asarray(inp[n], np.float32)) for n in WNAMES}
    w = shared["moba_wqkv"].reshape(2, D, 3, 8, 128).transpose(0, 3, 1, 2, 4)
    shared["moba_wqkv"] = np.ascontiguousarray(w).reshape(2, 8, D, 384)
    w = shared["hgrn_w_in"].reshape(1, D, 4, 8, 128).transpose(0, 3, 1, 2, 4)
    shared["hgrn_w_in"] = np.ascontiguousarray(w).reshape(1, 8, D, 512)
    w = shared["rg_w_in"].reshape(1, D, 2, 4, 256).transpose(0, 3, 1, 2, 4)
    shared["rg_w_in"] = np.ascontiguousarray(w).reshape(1, 4, D, 512)
    maps = []
    for core in range(n_cores):
        b0 = 2 * core
        m = dict(shared)
        m["xT"] = np.ascontiguousarray(x[b0:b0 + 2].transpose(0, 2, 1))
        m["consts"] = _consts(inp, b0)
        maps.append(m)
    return maps


def kernel(**inputs):
    nc = build()
    maps = make_in_maps(inputs)
    res = run_bass_kernel_spmd(nc, maps, core_ids=list(range(8)))
    out = np.empty((16, S, D), np.float32)
    for core in range(8):
        o = np.asarray(res.results[core]["outT"])
        out[2 * core:2 * core + 2] = o.transpose(0, 2, 1)
    return out
```
